# Optimizing a Trainium2 kernel written in Bass

```python
import math
import jax, jax.numpy as jnp
from jax import lax
import numpy as np

D_MODEL = 1024
BATCH = 2
SEQ = 8192
DEPTH = 2
DEC_BATCH = 128
DEC_SEQ = 4
PAST_LEN = 2048
PAGE_SIZE = 128

MIX_W = D_MODEL
ML_W = MIX_W // 2
ML_HEADS = 4
ML_HD = ML_W // ML_HEADS
ML_CHUNK = 64
NSA_W = MIX_W - ML_W
NSA_HEADS = 8
NSA_HD = NSA_W // NSA_HEADS
KV_HEADS = 2
Q_PER_KV = NSA_HEADS // KV_HEADS
KV_W = KV_HEADS * NSA_HD
CMP_BLOCK = 32
CMP_STRIDE = 16
CMP_HIDDEN = 256
SEL_BLOCK = 64
N_SELECT = 16
WINDOW = 512
Q_BLOCK = 128
NUM_BUCKETS = 32
REL_MAX_DIST = 128
D_FF = 4 * D_MODEL
ALPHA = (2 * DEPTH) ** 0.25
BETA = (8 * DEPTH) ** -0.25
LN_EPS = 1e-5
NEG = -1e30
FORCE = 1e9
IN_SIZES = (ML_W, ML_W, ML_W, ML_W, ML_HEADS, ML_HEADS, NSA_W, KV_W, KV_W, KV_W, KV_W, KV_W, KV_W, 3 * NSA_HEADS)
N_IN = sum(IN_SIZES)

kernel_name = 'hymba_mlstm_nsa_deepnorm_step'


def layer_norm(x, g, b):
    xf = x.astype(jnp.float32)
    mu = xf.mean(-1, keepdims=True)
    var = jnp.square(xf - mu).mean(-1, keepdims=True)
    return ((xf - mu) * lax.rsqrt(var + LN_EPS) * g + b).astype(x.dtype)


def head_norm(h, g):
    mu = h.mean(-1, keepdims=True)
    var = jnp.square(h - mu).mean(-1, keepdims=True)
    return (h - mu) * lax.rsqrt(var + LN_EPS) * g.reshape(ML_HEADS, ML_HD).astype(jnp.float32)


def rel_bucket(dist):
    n = jnp.maximum(dist, 0)
    max_exact = NUM_BUCKETS // 2
    nf = jnp.maximum(n, 1).astype(jnp.float32)
    large = max_exact + (jnp.log(nf / max_exact) / math.log(REL_MAX_DIST / max_exact)
                         * (NUM_BUCKETS - max_exact)).astype(jnp.int32)
    large = jnp.minimum(large, NUM_BUCKETS - 1)
    return jnp.where(n < max_exact, n, large)


def masked_softmax(s, mask):
    return jax.nn.softmax(jnp.where(mask, s, NEG), axis=-1) * mask


def mlstm_chunk(carry, inputs):
    C, n, m0 = carry
    q, k, v, ig, lf = inputs
    T = q.shape[1]
    F = jnp.cumsum(lf, axis=1)
    m = F + jnp.maximum(m0[:, None], lax.cummax(ig - F, axis=1))
    causal = (jnp.arange(T)[:, None] >= jnp.arange(T)[None, :])[None, :, :, None]
    log_d = F[:, :, None] - F[:, None, :] + ig[:, None, :] - m[:, :, None]
    dmat = jnp.exp(jnp.where(causal, log_d, NEG))
    w = jnp.einsum('bthd,bshd->btsh', q, k) * dmat
    decay = jnp.exp(F + m0[:, None] - m)
    num = jnp.einsum('btsh,bshd->bthd', w, v) + jnp.einsum('bthk,bhkv->bthv', q, C) * decay[..., None]
    den = w.sum(axis=2) + jnp.einsum('bthk,bhk->bth', q, n) * decay
    h = num / jnp.maximum(jnp.abs(den), jnp.exp(-m))[..., None]
    m_end = m[:, -1]
    f_end = F[:, -1]
    w_end = jnp.exp(f_end[:, None] - F + ig - m_end[:, None])
    carry_decay = jnp.exp(f_end + m0 - m_end)
    C_new = carry_decay[..., None, None] * C + jnp.einsum('bth,bthk,bthv->bhkv', w_end, k, v)
    n_new = carry_decay[..., None] * n + jnp.einsum('bth,bthk->bhk', w_end, k)
    return (C_new, n_new, m_end), h


def mlstm_prompt(q, k, v, ig, lf):
    B, S = q.shape[0], q.shape[1]
    nc = S // ML_CHUNK
    chunks = tuple(a.reshape(B, nc, ML_CHUNK, *a.shape[2:]).swapaxes(0, 1) for a in (q, k, v, ig, lf))
    carry0 = (jnp.zeros((B, ML_HEADS, ML_HD, ML_HD), jnp.float32),
              jnp.zeros((B, ML_HEADS, ML_HD), jnp.float32),
              jnp.zeros((B, ML_HEADS), jnp.float32))
    carry, h = lax.scan(mlstm_chunk, carry0, chunks)
    return h.swapaxes(0, 1).reshape(B, S, ML_HEADS, ML_HD), carry


def mlstm_sample(q, k, v, ig, lf, C, n, m):
    carry = (C.astype(jnp.float32), n.astype(jnp.float32), m.astype(jnp.float32))
    carry, h = mlstm_chunk(carry, (q, k, v, ig, lf))
    return h, carry


def compress_kv(kv, pe, w1, w2):
    L = kv.shape[1]
    n_cmp = (L - CMP_BLOCK) // CMP_STRIDE + 1
    idx = (jnp.arange(n_cmp) * CMP_STRIDE)[:, None] + jnp.arange(CMP_BLOCK)[None, :]
    blocks = kv[:, idx] + pe.transpose(1, 0, 2)[:, :, None, :]
    w1r = w1.reshape(2, CMP_BLOCK, NSA_HD, CMP_HIDDEN)
    hid = jax.nn.gelu(jnp.einsum('bncsgd,scdh->bnsgh', blocks, w1r))
    return jnp.einsum('bnsgh,shd->bnsgd', hid, w2)


def nsa_attend(q, qpos, kc, vc, ks, vs, kw, vw, wpos, gates, rel_bias):
    B, T = q.shape[0], q.shape[1]
    scale = NSA_HD ** -0.5
    tbl = rel_bias.reshape(NUM_BUCKETS, KV_HEADS, Q_PER_KV)
    n_cmp = kc.shape[1]
    c_start = jnp.arange(n_cmp) * CMP_STRIDE
    dist_c = qpos[:, None] - (c_start + CMP_BLOCK - 1)[None, :]
    bias_c = tbl[rel_bucket(dist_c)].transpose(0, 2, 3, 1)[None]
    s_c = jnp.einsum('btgrd,bngd->btgrn', q, kc).astype(jnp.float32) * scale + bias_c
    p_c = masked_softmax(s_c, (dist_c >= 0)[None, :, None, None, :])
    o_c = jnp.einsum('btgrn,bngd->btgrd', p_c.astype(vc.dtype), vc)
    n_sel = ks.shape[1] // SEL_BLOCK
    s_start = jnp.arange(n_sel) * SEL_BLOCK
    cover = ((c_start[:, None] < s_start[None, :] + SEL_BLOCK)
             & (c_start[:, None] + CMP_BLOCK > s_start[None, :])).astype(jnp.float32)
    imp = jnp.einsum('btgrn,nj->btgj', p_c, cover)
    cur = qpos // SEL_BLOCK
    jb = jnp.arange(n_sel)
    forced = (jb[None] == 0) | (jb[None] == cur[:, None]) | (jb[None] == cur[:, None] - 1)
    future = s_start[None, :] > qpos[:, None]
    score = jnp.where(forced[None, :, None, :], FORCE, jnp.where(future[None, :, None, :], -FORCE, imp))
    n_top = min(N_SELECT, n_sel)
    _, top = lax.top_k(score, n_top)
    pos = (top[..., None] * SEL_BLOCK + jnp.arange(SEL_BLOCK)).reshape(B, T, KV_HEADS, n_top * SEL_BLOCK)
    bi = jnp.arange(B)[:, None, None, None]
    gi = jnp.arange(KV_HEADS)[None, None, :, None]
    k_sel = ks[bi, pos, gi]
    v_sel = vs[bi, pos, gi]
    dist_s = qpos[None, :, None, None] - pos
    bias_s = tbl.transpose(1, 0, 2)[gi, rel_bucket(dist_s)]
    s_s = jnp.einsum('btgrd,btgkd->btgrk', q, k_sel).astype(jnp.float32) * scale + jnp.swapaxes(bias_s, -1, -2)
    p_s = masked_softmax(s_s, (dist_s >= 0)[:, :, :, None, :])
    o_s = jnp.einsum('btgrk,btgkd->btgrd', p_s.astype(v_sel.dtype), v_sel)
    dist_w = qpos[:, None] - wpos[None, :]
    valid_w = ((dist_w >= 0) & (dist_w < WINDOW) & (wpos[None, :] >= 0))[None, :, None, None, :]
    bias_w = tbl[rel_bucket(dist_w)].transpose(0, 2, 3, 1)[None]
    s_w = jnp.einsum('btgrd,bwgd->btgrw', q, kw).astype(jnp.float32) * scale + bias_w
    p_w = masked_softmax(s_w, valid_w)
    o_w = jnp.einsum('btgrw,bwgd->btgrd', p_w.astype(vw.dtype), vw)
    return gates[..., 0:1] * o_c + gates[..., 1:2] * o_s + gates[..., 2:3] * o_w


def nsa_prompt(q, kv_c, kv_s, kv_w, gates, pe, w1, w2, rel_bias):
    B, S = q.shape[0], q.shape[1]
    ckv = compress_kv(kv_c, pe, w1, w2)
    kwp = jnp.pad(kv_w, ((0, 0), (WINDOW, 0), (0, 0), (0, 0), (0, 0)))

    def query_block(blk):
        t0 = blk * Q_BLOCK
        qpos = t0 + jnp.arange(Q_BLOCK)
        kw = lax.dynamic_slice_in_dim(kwp, t0, WINDOW + Q_BLOCK, axis=1)
        wpos = t0 - WINDOW + jnp.arange(WINDOW + Q_BLOCK)
        return nsa_attend(lax.dynamic_slice_in_dim(q, t0, Q_BLOCK, axis=1), qpos,
                          ckv[:, :, 0], ckv[:, :, 1], kv_s[:, :, 0], kv_s[:, :, 1],
                          kw[:, :, 0], kw[:, :, 1], wpos,
                          lax.dynamic_slice_in_dim(gates, t0, Q_BLOCK, axis=1), rel_bias)

    o = lax.map(query_block, jnp.arange(S // Q_BLOCK))
    o = jnp.moveaxis(o, 0, 1).reshape(B, S, KV_HEADS, Q_PER_KV, NSA_HD)
    win = min(WINDOW, S)
    return o, (kv_c, kv_s, kv_w[:, S - win:])


def gather_pages(pool, page_table):
    rows = pool[page_table]
    return rows.reshape(page_table.shape[0], page_table.shape[1] * PAGE_SIZE, *pool.shape[2:])


def nsa_sample(q, kv_c, kv_s, kv_w, gates, pool_c, pool_s, win_buf, page_table, pe, w1, w2, rel_bias):
    T = q.shape[1]
    past = page_table.shape[1] * PAGE_SIZE
    full_c = jnp.concatenate([gather_pages(pool_c, page_table), kv_c], axis=1)
    ckv = compress_kv(full_c, pe, w1, w2)
    full_s = jnp.concatenate([gather_pages(pool_s, page_table), kv_s], axis=1)
    pad = -full_s.shape[1] % SEL_BLOCK
    full_s = jnp.pad(full_s, ((0, 0), (0, pad), (0, 0), (0, 0), (0, 0)))
    kw_all = jnp.concatenate([win_buf, kv_w], axis=1)
    wb = win_buf.shape[1]
    wpos = past - wb + jnp.arange(wb + T)
    qpos = past + jnp.arange(T)
    o = nsa_attend(q, qpos, ckv[:, :, 0], ckv[:, :, 1], full_s[:, :, 0], full_s[:, :, 1],
                   kw_all[:, :, 0], kw_all[:, :, 1], wpos, gates, rel_bias)
    return o, (kv_c, kv_s, kw_all[:, -wb:])


def trunk_layer(x, c, w_ada, b_ada, w_in, b_gate, ml_norm_g, w_out, ln_g, ln_b, w_up, w_down, mlstm_fn, nsa_fn):
    B, T = x.shape[0], x.shape[1]
    f32 = jnp.float32
    sh1, sc1, g1, sh2, sc2, g2 = jnp.split((jax.nn.silu(c) @ w_ada + b_ada)[:, None, :], 6, axis=-1)
    u = x * (1 + sc1) + sh1
    split_idx = np.cumsum(IN_SIZES)[:-1].tolist()
    mq, mk, mv, mo, mi, mf, nq, ck, cv, sk, sv, wk, wv, ng = jnp.split(u @ w_in, split_idx, axis=-1)
    heads = (B, T, ML_HEADS, ML_HD)
    q = mq.reshape(heads).astype(f32)
    k = mk.reshape(heads).astype(f32) * ML_HD ** -0.5
    v = mv.reshape(heads).astype(f32)
    ig = (mi + b_gate[:ML_HEADS]).astype(f32)
    lf = jax.nn.log_sigmoid((mf + b_gate[ML_HEADS:]).astype(f32))
    h_ml, ml_state = mlstm_fn(q, k, v, ig, lf)

    def kv_pair(a, b):
        return jnp.stack([a.reshape(B, T, KV_HEADS, NSA_HD), b.reshape(B, T, KV_HEADS, NSA_HD)], axis=2)

    nsa_q = nq.reshape(B, T, KV_HEADS, Q_PER_KV, NSA_HD)
    gates = jax.nn.sigmoid(ng).reshape(B, T, KV_HEADS, Q_PER_KV, 3)
    o_nsa, nsa_state = nsa_fn(nsa_q, kv_pair(ck, cv), kv_pair(sk, sv), kv_pair(wk, wv), gates)
    h_ml = (head_norm(h_ml, ml_norm_g) * jax.nn.sigmoid(mo.reshape(heads).astype(f32))).reshape(B, T, ML_W)
    mixed = jnp.concatenate([h_ml.astype(x.dtype), o_nsa.reshape(B, T, NSA_W).astype(x.dtype)], axis=-1) @ w_out
    x = layer_norm(ALPHA * x + g1 * mixed, ln_g[0], ln_b[0])
    u2 = x * (1 + sc2) + sh2
    ff = jnp.square(jax.nn.relu(u2 @ w_up)) @ w_down
    x = layer_norm(ALPHA * x + g2 * ff, ln_g[1], ln_b[1])
    return x, ml_state, nsa_state


def setup_inputs(seed: int = 0) -> dict:
    key = jax.random.key(seed)
    k = jax.random.split(key, 28)
    nrm = jax.random.normal
    f32 = jnp.float32
    D = D_MODEL
    n_pages = PAST_LEN // PAGE_SIZE
    n_used = DEC_BATCH * n_pages
    n_phys = (5 * n_used + 3) // 4
    win_len = min(WINDOW, PAST_LEN)
    page_table = jax.random.permutation(k[8], n_phys)[:n_used].reshape(DEC_BATCH, n_pages).astype(jnp.int32)
    gate_off = jnp.concatenate([jnp.zeros((2 * D,), f32), jnp.ones((D,), f32),
                                jnp.zeros((2 * D,), f32), jnp.ones((D,), f32)])
    forget_bias = jnp.linspace(3.0, 6.0, ML_HEADS, dtype=f32)
    return {
        'x_prompt': nrm(k[0], (BATCH, SEQ, D), f32),
        'x_sample': nrm(k[1], (DEC_BATCH, DEC_SEQ, D), f32),
        'cache_cmp_kv': nrm(k[2], (DEPTH, n_phys, PAGE_SIZE, 2, KV_HEADS, NSA_HD), f32),
        'cache_slc_kv': nrm(k[3], (DEPTH, n_phys, PAGE_SIZE, 2, KV_HEADS, NSA_HD), f32),
        'cache_win_kv': nrm(k[4], (DEPTH, DEC_BATCH, win_len, 2, KV_HEADS, NSA_HD), f32),
        'state_mlstm_C': 0.3 * nrm(k[5], (DEPTH, DEC_BATCH, ML_HEADS, ML_HD, ML_HD), f32),
        'state_mlstm_n': 0.3 * nrm(k[6], (DEPTH, DEC_BATCH, ML_HEADS, ML_HD), f32),
        'state_mlstm_m': jax.random.uniform(k[7], (DEPTH, DEC_BATCH, ML_HEADS), f32, 0.0, 3.0),
        'page_table': page_table,
        'c_prompt': nrm(k[9], (BATCH, D), f32),
        'c_sample': nrm(k[10], (DEC_BATCH, D), f32),
        'w_ada': 0.2 * D ** -0.5 * nrm(k[11], (DEPTH, D, 6 * D), f32),
        'b_ada': gate_off + 0.02 * nrm(k[12], (DEPTH, 6 * D), f32),
        'w_in': D ** -0.5 * nrm(k[13], (DEPTH, D, N_IN), f32),
        'b_gate': jnp.concatenate([0.1 * nrm(k[14], (DEPTH, ML_HEADS), f32),
                                   forget_bias + 0.1 * nrm(k[15], (DEPTH, ML_HEADS), f32)], axis=1),
        'ml_norm_g': 1.0 + 0.02 * nrm(k[16], (DEPTH, ML_W), f32),
        'cmp_pe': 0.1 * nrm(k[17], (DEPTH, 2, CMP_BLOCK, NSA_HD), f32),
        'cmp_w1': (CMP_BLOCK * NSA_HD) ** -0.5 * nrm(k[18], (DEPTH, 2, CMP_BLOCK * NSA_HD, CMP_HIDDEN), f32),
        'cmp_w2': CMP_HIDDEN ** -0.5 * nrm(k[19], (DEPTH, 2, CMP_HIDDEN, NSA_HD), f32),
        'rel_bias': 0.2 * nrm(k[20], (NUM_BUCKETS, NSA_HEADS), f32),
        'w_out': BETA * MIX_W ** -0.5 * nrm(k[21], (DEPTH, MIX_W, D), f32),
        'ln_g': 1.0 + 0.02 * nrm(k[22], (DEPTH, 2, D), f32),
        'ln_b': 0.02 * nrm(k[23], (DEPTH, 2, D), f32),
        'w_up': D ** -0.5 * nrm(k[24], (DEPTH, D, D_FF), f32),
        'w_down': BETA * D_FF ** -0.5 * nrm(k[25], (DEPTH, D_FF, D), f32),
    }


def reference(x_prompt, x_sample, cache_cmp_kv, cache_slc_kv, cache_win_kv, state_mlstm_C, state_mlstm_n,
              state_mlstm_m, page_table, c_prompt, c_sample, w_ada, b_ada, w_in, b_gate, ml_norm_g, cmp_pe,
              cmp_w1, cmp_w2, rel_bias, w_out, ln_g, ln_b, w_up, w_down):
    xp, xs = x_prompt, x_sample
    cmp_p, cmp_s, slc_p, slc_s, win_p, win_s = [], [], [], [], [], []
    C_p, C_s, n_p, n_s, m_p, m_s = [], [], [], [], [], []
    for l in range(DEPTH):
        lw = (w_ada[l], b_ada[l], w_in[l], b_gate[l], ml_norm_g[l], w_out[l], ln_g[l], ln_b[l], w_up[l], w_down[l])
        cmp_l = (cmp_pe[l], cmp_w1[l], cmp_w2[l])
        xp, mlp_state, nsp_state = trunk_layer(
            xp, c_prompt, *lw, mlstm_prompt,
            lambda *a: nsa_prompt(*a, *cmp_l, rel_bias))
        ml_s = lambda *a: mlstm_sample(*a, state_mlstm_C[l], state_mlstm_n[l], state_mlstm_m[l])
        nsa_s = lambda *a: nsa_sample(*a, cache_cmp_kv[l], cache_slc_kv[l], cache_win_kv[l], page_table,
                                      *cmp_l, rel_bias)
        xs, mls_state, nss_state = trunk_layer(xs, c_sample, *lw, ml_s, nsa_s)
        cmp_p.append(nsp_state[0]); slc_p.append(nsp_state[1]); win_p.append(nsp_state[2])
        cmp_s.append(nss_state[0]); slc_s.append(nss_state[1]); win_s.append(nss_state[2])
        C_p.append(mlp_state[0]); n_p.append(mlp_state[1]); m_p.append(mlp_state[2])
        C_s.append(mls_state[0]); n_s.append(mls_state[1]); m_s.append(mls_state[2])
    return (xp, xs,
            jnp.stack(cmp_p), jnp.stack(cmp_s),
            jnp.stack(slc_p), jnp.stack(slc_s),
            jnp.stack(win_p), jnp.stack(win_s),
            jnp.stack(C_p), jnp.stack(C_s),
            jnp.stack(n_p), jnp.stack(n_s),
            jnp.stack(m_p), jnp.stack(m_s))
```

```python
import math
import numpy as np
from contextlib import ExitStack
import concourse.bass as bass
import concourse.mybir as mybir
from concourse.bass_utils import run_bass_kernel_spmd

F32 = mybir.dt.float32
BF16 = mybir.dt.bfloat16
I32 = mybir.dt.int32
AF = mybir.ActivationFunctionType
ALU = mybir.AluOpType
AX = mybir.AxisListType

D_MODEL = 1024
N_IN = 3360
NEG = -30000.0
ALPHA = (2 * 2) ** 0.25
LN_EPS = 1e-5
import os
N_CORES = int(os.environ.get('DBG_CORES', '8'))


class Res:
    def __init__(self, name, t=None):
        self.name = name
        self.t = t
        self.w = None
        self.r = {}
        self.dsem = None
        self.dcnt = 0
        self.slot = None

    def __getitem__(self, k):
        return self.t[k]


class Ctx:
    def __init__(self, nc, es):
        self.nc = nc
        self.es = es
        self.engs = {"pe": nc.tensor, "dve": nc.vector, "act": nc.scalar, "pool": nc.gpsimd, "sp": nc.sync}
        self.sem = {k: es.enter_context(nc.semaphore("s_" + k)) for k in ["pe", "dve", "act", "pool"]}
        self.cnt = {k: 0 for k in self.sem}
        self.waited = {k: {} for k in self.engs}
        self.dpool = []
        self.psb = []
        self.psi = 0
        self.ninst = 0
        self.phase_owners = []

    def sb(self, st, name, shape, dt=F32):
        self.nsb = getattr(self, "nsb", 0) + 1
        name = "%s_%d" % (name, self.nsb)
        t = st.enter_context(self.nc.sbuf_tensor(name, list(shape), dt))
        return Res(name, t)

    def dram(self, name, shape, dt=F32, kind="Internal"):
        t = self.nc.dram_tensor(name, list(shape), dt, kind=kind).ap()
        return Res(name, t)

    def init_psum(self):
        for i in range(8):
            t = self.es.enter_context(self.nc.psum_tensor("psb%d" % i, [128, 512], F32))
            self.psb.append(Res("psb%d" % i, t))
            self.psb[-1].excl = True

    def ps(self):
        p = self.psb[self.psi]
        self.psi = (self.psi + 1) % 6
        return p

    def ps_acc(self):
        self.pai = 1 - getattr(self, "pai", 0)
        return self.psb[6 + self.pai]

    def _semof(self, key):
        return self.sem[key] if isinstance(key, str) else self.dpool[key.slot][0]

    def _need(self, eng, deps):
        e = self.engs[eng]
        best = {}
        for key, val in deps:
            if key == "pe" and eng == "pe":
                continue
            if not isinstance(key, str) and key.slot is None:
                continue
            kid = key if isinstance(key, str) else ("d", key.slot)
            if self.waited[eng].get(kid, 0) >= val:
                continue
            if kid not in best or best[kid][1] < val:
                best[kid] = (key, val)
        for kid, (key, val) in best.items():
            e.wait_ge(self._semof(key), val)
            self.waited[eng][kid] = val

    @staticmethod
    def _deps(reads, writes, partial_owner=None):
        deps = []
        for r in reads:
            if r.w is not None:
                deps.append(r.w)
            if getattr(r, "excl", False):
                deps.extend(r.r.items())
        for w in writes:
            if w.w is not None and not (partial_owner is not None and w.w[0] is partial_owner):
                deps.append(w.w)
            deps.extend(w.r.items())
        return deps

    def op(self, eng, fn, reads=(), writes=()):
        self._need(eng, self._deps(reads, writes))
        inst = fn(self.engs[eng])
        self.cnt[eng] += 1
        c = self.cnt[eng]
        inst.then_inc(self.sem[eng], 1)
        self.ninst += 1
        for r in reads:
            r.r[eng] = c
        for w in writes:
            w.w = (eng, c)
            w.r = {}
        return inst

    def _own(self, owner):
        if owner.slot is None:
            for i, s in enumerate(self.dpool):
                if not s[2]:
                    owner.slot = i
                    s[2] = True
                    break
            else:
                sem = self.es.enter_context(self.nc.semaphore("d%d" % len(self.dpool)))
                self.dpool.append([sem, 0, True])
                owner.slot = len(self.dpool) - 1
            self.phase_owners.append(owner)

    def dma(self, q, out_ap, in_ap, reads=(), writes=(), owner=None, partial=False, fn=None):
        self._own(owner)
        self._need(q, self._deps(reads, writes, owner if partial else None))
        if fn is None:
            inst = self.engs[q].dma_start(out=out_ap, in_=in_ap)
        else:
            inst = fn(self.engs[q])
        s = self.dpool[owner.slot]
        s[1] += 16
        inst.then_inc(s[0], 16)
        self.ninst += 1
        for r in reads:
            r.r[owner] = s[1]
        for w in writes:
            w.w = (owner, s[1])
            w.r = {}
        return inst

    def barrier(self, release=True):
        for eng in self.engs:
            e = self.engs[eng]
            for k in self.sem:
                if k == eng:
                    continue
                if self.cnt[k] > self.waited[eng].get(k, 0):
                    e.wait_ge(self.sem[k], self.cnt[k])
                    self.waited[eng][k] = self.cnt[k]
            for i, s in enumerate(self.dpool):
                kid = ("d", i)
                if s[1] > self.waited[eng].get(kid, 0):
                    e.wait_ge(s[0], s[1])
                    self.waited[eng][kid] = s[1]
        if release:
            for o in self.phase_owners:
                self.dpool[o.slot][2] = False
                o.slot = None
            self.phase_owners = []

    def mm(self, out, lhsT, rhs, start, stop, reads, writes):
        return self.op("pe", lambda e: e.matmul(out, lhsT, rhs, start=start, stop=stop), reads, writes)

    def tr(self, out, in_, ident, reads, writes):
        return self.op("pe", lambda e: e.transpose(out, in_, ident), reads, writes)


def rel_bucket_np(dist):
    n = np.maximum(dist, 0)
    nf = np.maximum(n, 1).astype(np.float32)
    large = 16 + (np.log(nf / 16) / math.log(128 / 16) * 16).astype(np.int32)
    large = np.minimum(large, 31)
    return np.where(n < 16, n, large)


def host_consts():
    k = {}
    tl = np.arange(128)[:, None]
    kl = np.arange(128)[None, :]
    k["ident"] = np.eye(128, dtype=np.float32)
    k["maskT"] = ((tl <= kl) & (tl // 64 == kl // 64)).astype(np.float32)
    k["ones"] = np.ones((128, 128), np.float32)
    k["chind"] = (np.arange(128)[:, None] // 64 == np.arange(2)[None, :]).astype(np.float32)
    k["m0"] = np.where(kl <= tl, 0.0, NEG).astype(np.float32)
    k["m4"] = np.where(kl <= tl, NEG, 0.0).astype(np.float32)
    npr = 15 - np.arange(16)[None, :]
    k["mc16"] = np.where(16 * npr + tl - 143 < 0, NEG, 0.0).astype(np.float32)
    k["jt"] = (kl - (tl >= 64)).astype(np.float32)
    k["js"] = np.broadcast_to(np.arange(128, dtype=np.float32)[None, :], (128, 128)).copy()
    k["nz"] = np.broadcast_to((np.arange(128) >= 1).astype(np.float32)[None, :], (128, 128)).copy()
    k["rv0"] = (np.arange(128) >= 31).astype(np.float32)[:, None].copy()
    k["z0"] = np.broadcast_to((np.arange(128) == 0).astype(np.float32)[None, :], (128, 128)).copy()
    return k


def bias_tiles(rel_bias):
    tl = np.arange(128)[:, None]
    kl = np.arange(128)[None, :]
    b0 = rel_bias[rel_bucket_np(tl - kl)]
    b1 = rel_bias[rel_bucket_np(128 + tl - kl)]
    npr = 15 - np.arange(16)[None, :]
    bc = rel_bias[rel_bucket_np(16 * npr + tl - 143)]
    bcs = rel_bias[rel_bucket_np(16 * npr + tl + 1)]
    c31 = np.broadcast_to(rel_bias[31][None, :], (128, 8))
    f = lambda a: np.ascontiguousarray(np.moveaxis(a, -1, 1)).astype(np.float32)
    return {"b0": f(b0), "b1": f(b1), "bc": f(bc), "bcs": f(bcs), "c31": np.ascontiguousarray(c31).astype(np.float32)}


def build(SEQ, NSEQ, NPHYS, dbg=False, stop_after=None):
    NT = SEQ // 128
    NS = 1 + NSEQ
    NTOK = SEQ + 4 * NSEQ
    nc = bass.Bass("TRN2", target_bir_lowering=False)
    es = ExitStack()
    with es:
        c = Ctx(nc, es)
        c.init_psum()
        D = {}

        def din(name, shape, dt=F32):
            D[name] = c.dram(name, shape, dt, "ExternalInput")

        def dout(name, shape, dt=F32):
            D[name] = c.dram(name, shape, dt, "ExternalOutput")

        def dscr(name, shape, dt=F32):
            D[name] = c.dram(name, shape, dt, "ExternalOutput" if dbg else "Internal")

        din("xp", [SEQ, 1024]); din("xs", [NSEQ * 4, 1024])
        din("pool_c", [2, NPHYS * 128, 256]); din("pool_s", [2, NPHYS * 128, 256])
        din("win_c", [2, NSEQ, 512, 256])
        din("stC", [2, NSEQ, 4, 128, 128]); din("stn", [2, NSEQ, 4, 128]); din("stm", [2, NSEQ, 4])
        din("ptab", [NSEQ, 16], I32)
        din("c_all", [NS, 1024])
        din("w_ada", [2, 1024, 6144]); din("b_ada", [2, 6144]); din("w_in", [2, 1024, N_IN])
        din("b_gate", [2, 8]); din("ml_g", [2, 512]); din("cmp_pe", [2, 2, 2048])
        din("cmp_w1", [2, 2, 2048, 256]); din("cmp_w2", [2, 2, 256, 64])
        din("w_out", [2, 1024, 1024]); din("ln_g", [2, 2, 1024]); din("ln_b", [2, 2, 1024])
        din("w_up", [2, 1024, 4096]); din("w_down", [2, 4096, 1024])
        for nm, shp in [("ident", [128, 128]), ("maskT", [128, 128]), ("ones", [128, 128]), ("chind", [128, 2]),
                        ("m0", [128, 128]), ("m4", [128, 128]), ("mc16", [128, 16]), ("jt", [128, 128]),
                        ("js", [128, 128]), ("nz", [128, 128]), ("z0", [128, 128]), ("rv0", [128, 1]),
                        ("b0", [128, 8, 128]), ("b1", [128, 8, 128]), ("bc", [128, 8, 16]), ("bcs", [128, 8, 16]),
                        ("c31", [128, 8])]:
            din(nm, shp)
        dout("yp", [SEQ, 1024]); dout("ys", [NSEQ * 4, 1024])
        dout("cmp_p", [2, SEQ, 256]); dout("slc_p", [2, SEQ, 256]); dout("win_p", [2, min(512, SEQ), 256])
        dout("cmp_s", [2, NSEQ * 4, 256]); dout("slc_s", [2, NSEQ * 4, 256]); dout("win_s", [2, NSEQ, 512, 256])
        dout("C_p", [2, 4, 128, 128]); dout("n_p", [2, 4, 128]); dout("m_p", [2, 4])
        dout("C_s", [2, NSEQ, 4, 128, 128]); dout("n_s", [2, NSEQ, 4, 128]); dout("m_s", [2, NSEQ, 4])
        dscr("x1", [NTOK, 1024]); dscr("xmid", [NTOK, 1024])
        dscr("hml", [NTOK, 512], BF16); dscr("mod", [2, NS, 6144])
        if dbg:
            dout("onsa", [NTOK, 512], BF16)
        dscr("w_in_b", [2, 1024, N_IN], BF16); dscr("w_out_b", [2, 1024, 1024], BF16)
        dscr("w_up_b", [2, 1024, 4096], BF16); dscr("w_down_b", [2, 4096, 1024], BF16)
        dscr("w1_b", [2, 2, 2048, 256], BF16); dscr("w2_b", [2, 2, 256, 64], BF16); dscr("pe_b", [2, 2, 2048], BF16)

        K = {}
        for nm, shp in [("ident", [128, 128]), ("maskT", [128, 128]), ("ones", [128, 128]), ("chind", [128, 2]),
                        ("m0", [128, 128]), ("m4", [128, 128]), ("mc16", [128, 16]), ("jt", [128, 128]),
                        ("js", [128, 128]), ("nz", [128, 128]), ("z0", [128, 128]), ("rv0", [128, 1]),
                        ("b0", [128, 8, 128]), ("b1", [128, 8, 128]), ("bc", [128, 8, 16]), ("bcs", [128, 8, 16]),
                        ("c31", [128, 8])]:
            K[nm] = c.sb(es, "k_" + nm, shp)
            c.dma("sp", K[nm][:], D[nm][:], reads=[D[nm]], writes=[K[nm]], owner=K[nm])
        K["identb"] = c.sb(es, "k_identb", [128, 128], BF16)
        c.op("dve", lambda e: e.tensor_copy(K["identb"][:], K["ident"][:]), [K["ident"]], [K["identb"]])
        for h in range(8):
            for nm, mk in [("b0", "m0"), ("b1", None), ("bc", "mc16"), ("bcs", None)]:
                t = K[nm]
                c.op("dve", lambda e, t=t, h=h: e.tensor_scalar(t[:, h, :], t[:, h, :], K["c31"][:, h:h + 1], None, ALU.subtract),
                     [t, K["c31"]], [t])
                if mk is not None:
                    c.op("dve", lambda e, t=t, h=h, mk=mk: e.tensor_tensor(t[:, h, :], t[:, h, :], K[mk][:], ALU.add), [t, K[mk]], [t])

        with ExitStack() as ph:
            stf = [c.sb(ph, "stf%d" % i, [128, 2048]) for i in range(3)]
            stb = [c.sb(ph, "stb%d" % i, [128, 2048], BF16) for i in range(3)]
            it = 0
            jobs = []
            for l in range(2):
                jobs.append((D["w_in"][l], D["w_in_b"][l], 1024, N_IN))
                jobs.append((D["w_out"][l], D["w_out_b"][l], 1024, 1024))
                jobs.append((D["w_up"][l], D["w_up_b"][l], 1024, 4096))
                jobs.append((D["w_down"][l], D["w_down_b"][l], 4096, 1024))
                for s in range(2):
                    jobs.append((D["cmp_w1"][l, s], D["w1_b"][l, s], 2048, 256))
                    jobs.append((D["cmp_w2"][l, s], D["w2_b"][l, s], 256, 64))
            srcs = [D[n] for n in ["w_in", "w_out", "w_up", "w_down", "cmp_w1", "cmp_w2"]]
            dsts = [D[n] for n in ["w_in_b", "w_out_b", "w_up_b", "w_down_b", "w1_b", "w2_b"]]
            for (src, dst, R, C) in jobs:
                for r0 in range(0, R, 128):
                    for c0 in range(0, C, 2048):
                        w = min(2048, C - c0)
                        a, b = stf[it % 3], stb[it % 3]
                        c.dma("sp", a[:, 0:w], src[r0:r0 + 128, c0:c0 + w], reads=srcs, writes=[a], owner=a)
                        eng = ["dve", "pool", "act"][it % 3]
                        if eng == "act":
                            c.op("act", lambda e, a=a, b=b, w=w: e.activation(b[:, 0:w], a[:, 0:w], AF.Copy), [a], [b])
                        else:
                            c.op(eng, lambda e, a=a, b=b, w=w: e.tensor_copy(b[:, 0:w], a[:, 0:w]), [a], [b])
                        c.dma("pool", dst[r0:r0 + 128, c0:c0 + w], b[:, 0:w], reads=[b], writes=dsts, owner=b)
                        it += 1
            a, b = stf[0], stb[0]
            c.dma("sp", a[0:4, 0:2048], D["cmp_pe"][:].rearrange("l s f -> (l s) f"), reads=[D["cmp_pe"]], writes=[a], owner=a)
            c.op("dve", lambda e: e.tensor_copy(b[0:4, 0:2048], a[0:4, 0:2048]), [a], [b])
            c.dma("pool", D["pe_b"][:].rearrange("l s f -> (l s) f"), b[0:4, 0:2048], reads=[b], writes=[D["pe_b"]], owner=b)
            c.barrier()

        with ExitStack() as ph:
            ct = c.sb(ph, "ct", [NS, 1024])
            scT = c.sb(ph, "scT", [128, 8, NS])
            wst = [c.sb(ph, "wst%d" % i, [128, 8, 256]) for i in range(2)]
            bst = [c.sb(ph, "bst%d" % i, [1, 256]) for i in range(2)]
            mo = [c.sb(ph, "mo%d" % i, [NS, 256]) for i in range(2)]
            c.dma("sp", ct[:], D["c_all"][:], reads=[D["c_all"]], writes=[ct], owner=ct)
            c.op("act", lambda e: e.activation(ct[:], ct[:], AF.Silu), [ct], [ct])
            p = c.ps()
            for kt in range(8):
                c.tr(p[:, kt * NS:(kt + 1) * NS], ct[:, kt * 128:(kt + 1) * 128], K["ident"][:NS, :NS], [ct, K["ident"]], [p])
            c.op("dve", lambda e: e.tensor_copy(scT[:].rearrange("p k n -> p (k n)"), p[:, 0:8 * NS]), [p], [scT])
            it = 0
            for l in range(2):
                for j in range(24):
                    a, bb, m = wst[it % 2], bst[it % 2], mo[it % 2]
                    c.dma("sp", a[:], D["w_ada"][l, :, j * 256:(j + 1) * 256].rearrange("(k p) n -> p k n", p=128),
                          reads=[D["w_ada"]], writes=[a], owner=a)
                    c.dma("sp", bb[:], D["b_ada"][l:l + 1, j * 256:(j + 1) * 256], reads=[D["b_ada"]], writes=[bb], owner=bb)
                    p = c.ps()
                    for kt in range(8):
                        c.mm(p[:NS, 0:256], scT[:, kt, :], a[:, kt, :], kt == 0, False, [scT, a], [p])
                    c.mm(p[:NS, 0:256], K["ones"][0:1, :NS], bb[:], False, True, [K["ones"], bb], [p])
                    sec = (j * 256) // 1024
                    add1 = 1.0 if sec in (1, 4) else 0.0
                    c.op("dve", lambda e, m=m, p=p, add1=add1: e.tensor_scalar(m[:], p[:NS, 0:256], add1, None, ALU.add), [p], [m])
                    c.dma("pool", D["mod"][l, :, j * 256:(j + 1) * 256], m[:], reads=[m], writes=[D["mod"]], owner=m)
                    it += 1
            c.barrier()
        if stop_after == "ada":
            c.barrier(release=False)
            return nc

        ptiles = [(i * 128, 128, 0, i) for i in range(NT)]
        stiles = [(SEQ + 4 * s, 4, 1 + s, 0) for s in range(NSEQ)]

        def load_mod(tile, l, sec, sidx, Q):
            c.dma("sp", tile[:Q, :], D["mod"][l, sidx:sidx + 1, sec * 1024:(sec + 1) * 1024].partition_broadcast(Q),
                  reads=[D["mod"]], writes=[tile], owner=tile)

        def layer_norm(st, pre, Q, lng, lnb, out, tmp):
            for hh in range(2):
                c.op("dve", lambda e, hh=hh: e.bn_stats(st[:Q, hh * 6:(hh + 1) * 6], pre[:Q, hh * 512:(hh + 1) * 512]), [pre], [st])
            c.op("dve", lambda e: e.bn_aggr(st[:Q, 12:14], st[:Q, 0:12]), [st], [st])
            c.op("act", lambda e: e.activation(st[:Q, 14:15], st[:Q, 13:14], AF.Sqrt, bias=K["eps"][:Q, 0:1], scale=1.0), [st, K["eps"]], [st])
            c.op("dve", lambda e: e.reciprocal(st[:Q, 15:16], st[:Q, 14:15]), [st], [st])
            c.op("dve", lambda e: e.tensor_scalar(tmp[:Q, :], pre[:Q, :], st[:Q, 12:13], st[:Q, 15:16], ALU.subtract, ALU.mult), [pre, st], [tmp])
            c.op("dve", lambda e: e.tensor_tensor(tmp[:Q, :], tmp[:Q, :], lng[:Q, :], ALU.mult), [tmp, lng], [tmp])
            c.op("dve", lambda e: e.tensor_tensor(out[:Q, :], tmp[:Q, :], lnb[:Q, :], ALU.add), [tmp, lnb], [out])

        K["eps"] = c.sb(es, "k_eps", [128, 1])
        c.op("dve", lambda e: e.memset(K["eps"][:], LN_EPS), [], [K["eps"]])

        for l in range(2):
            phase_a1(c, nc, D, K, l, SEQ, NSEQ, ptiles, stiles, load_mod, dbg)
            if stop_after == ("a1", l):
                c.barrier(release=False)
                return nc
            phase_a2(c, nc, D, K, l, SEQ, NSEQ, NPHYS, ptiles, stiles, load_mod, layer_norm, dbg)
            if stop_after == ("a2", l):
                c.barrier(release=False)
                return nc
            phase_b(c, nc, D, K, l, SEQ, NSEQ, ptiles, stiles, load_mod, layer_norm, dbg)
            if stop_after == ("b", l):
                c.barrier(release=False)
                return nc
        c.barrier(release=False)
    return nc


def xsrc(D, l, row0, Q, sidx, SEQ):
    if l == 0:
        return D["xp"][row0:row0 + Q, :] if sidx == 0 else D["xs"][row0 - SEQ:row0 - SEQ + Q, :]
    return D["x1"][row0:row0 + Q, :]


def make_uT(c, K, x, Q, MODsh, MODsc, ub, uT):
    c.op("dve", lambda e: e.tensor_tensor(ub[:Q, :], x[:Q, :], MODsc[:Q, :], ALU.mult), [x, MODsc], [ub])
    c.op("dve", lambda e: e.tensor_tensor(ub[:Q, :], ub[:Q, :], MODsh[:Q, :], ALU.add), [ub, MODsh], [ub])
    p = c.ps()
    pb = p[:, :].bitcast(BF16)
    for kt in range(8):
        c.tr(pb[:, kt * Q:(kt + 1) * Q], ub[:Q, kt * 128:(kt + 1) * 128], K["identb"][:Q, :Q], [ub, K["identb"]], [p])
    c.op("act", lambda e: e.activation(uT[:, :, :Q], pb[:, 0:8 * Q].rearrange("p (k q) -> p k q", q=Q), AF.Copy), [p], [uT])


def phase_a1(c, nc, D, K, l, SEQ, NSEQ, ptiles, stiles, load_mod, dbg):
    SC = 128 ** -0.5
    with ExitStack() as ph:
        win = c.sb(ph, "win1", [128, 8, 2056], BF16)
        for kt in range(8):
            c.dma("sp", win[:, kt, :], D["w_in_b"][l, kt * 128:(kt + 1) * 128, 0:2056], writes=[win], owner=win, partial=True)
        MODsh = c.sb(ph, "a1sh", [128, 1024]); MODsc = c.sb(ph, "a1sc", [128, 1024])
        gml = c.sb(ph, "gml", [128, 512]); bgt = c.sb(ph, "bgt", [128, 8]); bgf = c.sb(ph, "bgf", [4, 2]); nbf = c.sb(ph, "nbf", [4, 1])
        c.dma("sp", gml[:], D["ml_g"][l:l + 1, :].partition_broadcast(128), writes=[gml], owner=gml)
        c.dma("sp", bgt[:], D["b_gate"][l:l + 1, :].partition_broadcast(128), writes=[bgt], owner=bgt)
        for a in range(2):
            c.dma("sp", bgf[:, a:a + 1], D["b_gate"][l, a * 4:(a + 1) * 4].rearrange("(h o) -> h o", o=1), writes=[bgf], owner=bgf, partial=True)
        c.op("dve", lambda e: e.tensor_scalar(nbf[:], bgf[:, 1:2], -1.0, None, ALU.mult), [bgf], [nbf])
        xt = [c.sb(ph, "a1x%d" % i, [128, 1024]) for i in range(2)]
        ub = c.sb(ph, "a1ub", [128, 1024], BF16); uT = c.sb(ph, "a1uT", [128, 8, 128], BF16)
        qT = c.sb(ph, "qT", [128, 4, 128], BF16); kT = c.sb(ph, "kT", [128, 4, 128], BF16)
        ktm = c.sb(ph, "ktm", [128, 512], BF16); vaug = c.sb(ph, "vaug", [128, 4, 129], BF16); sgo = c.sb(ph, "sgo", [128, 512])
        g8 = c.sb(ph, "g8", [128, 8]); lf = c.sb(ph, "lf", [128, 4]); av = c.sb(ph, "av", [128, 4]); ea = c.sb(ph, "ea", [128, 4])
        emF = c.sb(ph, "emF", [128, 4]); lfblk = c.sb(ph, "lfblk", [128, 8]); EF = c.sb(ph, "EF", [128, 8])
        igT = c.sb(ph, "igT", [4, 128]); lfT = c.sb(ph, "lfT", [4, 128]); mT = c.sb(ph, "mT", [4, 128]); mprev = c.sb(ph, "mprev", [4, 1])
        S = [c.sb(ph, "S%d" % h, [128, 129]) for h in range(4)]
        tmpS = [c.sb(ph, "tmpS%d" % h, [128, 129]) for h in range(4)]
        Cb = [[c.sb(ph, "Cb%d_%d" % (k, h), [128, 129], BF16) for h in range(4)] for k in range(3)]
        wT = [c.sb(ph, "wT%d" % h, [128, 128], BF16) for h in range(4)]
        hn = [c.sb(ph, "hn%d" % h, [128, 128]) for h in range(4)]
        dn = [c.sb(ph, "dn%d" % h, [128, 4]) for h in range(4)]
        stt = [c.sb(ph, "stt%d" % h, [128, 10]) for h in range(4)]
        hmlt = [c.sb(ph, "hmlt%d" % i, [128, 512], BF16) for i in range(2)]
        EM = c.sb(ph, "EM", [128, 4]); cst = c.sb(ph, "cst", [128, 4, 129]); nst = c.sb(ph, "nst", [4, 128]); nld = c.sb(ph, "nld", [4, 128])

        tiles = ptiles + stiles

        def load_x(k):
            row0, Q, sidx, ti = tiles[k]
            c.dma("sp", xt[k % 2][:Q, :], xsrc(D, l, row0, Q, sidx, SEQ), writes=[xt[k % 2]], owner=xt[k % 2])

        def bcast_m(scale):
            p = c.ps()
            c.mm(p[:, 0:4], mprev[:, 0:1].to_broadcast([4, 128]), K["ident"][:4, :4], True, True, [mprev, K["ident"]], [p])
            c.op("act", lambda e: e.activation(EM[:], p[:, 0:4], AF.Exp, scale=scale), [p], [EM])

        load_x(0)
        gc = 0
        cur_seq = None
        for k, (row0, Q, sidx, ti) in enumerate(tiles):
            if k + 1 < len(tiles):
                load_x(k + 1)
            nch = 2 if Q == 128 else 1
            T = Q // nch
            x = xt[k % 2]
            if sidx != cur_seq:
                cur_seq = sidx
                gc = 0
                load_mod(MODsh, l, 0, sidx, Q)
                load_mod(MODsc, l, 1, sidx, Q)
                if sidx == 0:
                    for h in range(4):
                        c.op("pool", lambda e, h=h: e.memset(S[h][:], 0.0), [], [S[h]])
                        c.op("pool", lambda e, h=h: e.memset(Cb[0][h][:], 0.0), [], [Cb[0][h]])
                    c.op("pool", lambda e: e.memset(mprev[:], 0.0), [], [mprev])
                else:
                    s = sidx - 1
                    for h in range(4):
                        c.dma("sp", S[h][:, 0:128], D["stC"][l, s, h], writes=[S[h]], owner=S[h])
                    c.dma("sp", nld[:], D["stn"][l, s], writes=[nld], owner=nld)
                    c.dma("sp", mprev[:], D["stm"][l, s, :].rearrange("(h o) -> h o", o=1), writes=[mprev], owner=mprev)
                    p = c.ps()
                    c.tr(p[:, 0:4], nld[:, :], K["ident"][:4, :4], [nld, K["ident"]], [p])
                    for h in range(4):
                        c.op("dve", lambda e, h=h, p=p: e.tensor_copy(S[h][:, 128:129], p[:, h:h + 1]), [p], [S[h]])
                    bcast_m(1.0)
                    for h in range(4):
                        c.op("dve", lambda e, h=h: e.tensor_scalar(S[h][:], S[h][:], EM[:, h:h + 1], None, ALU.mult), [S[h], EM], [S[h]])
                        c.op("act", lambda e, h=h: e.activation(Cb[0][h][:], S[h][:], AF.Copy), [S[h]], [Cb[0][h]])
            make_uT(c, K, x, Q, MODsh, MODsc, ub, uT)
            for which, col0, dst, scl in [(0, 0, qT, 1.0), (1, 512, kT, SC)]:
                p = c.ps()
                for h in range(4):
                    for kt in range(8):
                        c.mm(p[:, h * Q:(h + 1) * Q], win[:, kt, col0 + h * 128:col0 + (h + 1) * 128], uT[:, kt, :Q], kt == 0, kt == 7, [win, uT], [p])
                c.op("act", lambda e, p=p, dst=dst, scl=scl: e.activation(dst[:, :, :Q], p[:, 0:4 * Q].rearrange("p (h q) -> p h q", q=Q), AF.Copy, scale=scl), [p], [dst])
            p = c.ps()
            for a in range(2):
                for kt in range(8):
                    c.mm(p[0:4, a * Q:(a + 1) * Q], win[:, kt, 2048 + 4 * a:2052 + 4 * a], uT[:, kt, :Q], kt == 0, kt == 7, [win, uT], [p])
            c.op("dve", lambda e, p=p: e.tensor_scalar(igT[:, :Q], p[0:4, 0:Q], bgf[:, 0:1], None, ALU.add), [p, bgf], [igT])
            c.op("act", lambda e, p=p: e.activation(lfT[:, :Q], p[0:4, Q:2 * Q], AF.Exp, bias=nbf[:, 0:1], scale=-1.0), [p, nbf], [lfT])
            c.op("act", lambda e: e.activation(lfT[:, :Q], lfT[:, :Q], AF.Ln, bias=1.0, scale=1.0), [lfT], [lfT])
            c.op("dve", lambda e: e.tensor_scalar(lfT[:, :Q], lfT[:, :Q], -1.0, None, ALU.mult), [lfT], [lfT])
            c.op("dve", lambda e: e.tensor_tensor_scan(mT[:, :Q], lfT[:, :Q], igT[:, :Q], mprev[:, 0:1], ALU.add, ALU.max), [lfT, igT, mprev], [mT])
            c.op("dve", lambda e: e.tensor_copy(mprev[:, 0:1], mT[:, Q - 1:Q]), [mT], [mprev])
            pk = c.ps()
            for kt in range(8):
                c.mm(pk[:Q, :], uT[:, kt, :Q], win[:, kt, 512:1024], kt == 0, kt == 7, [uT, win], [pk])
            c.op("act", lambda e, pk=pk: e.activation(ktm[:Q, :], pk[:Q, :], AF.Copy, scale=SC), [pk], [ktm])
            pv = c.ps()
            for kt in range(8):
                c.mm(pv[:Q, :], uT[:, kt, :Q], win[:, kt, 1024:1536], kt == 0, kt == 7, [uT, win], [pv])
            po = c.ps()
            for kt in range(8):
                c.mm(po[:Q, :], uT[:, kt, :Q], win[:, kt, 1536:2048], kt == 0, kt == 7, [uT, win], [po])
            c.op("act", lambda e, po=po: e.activation(sgo[:Q, :], po[:Q, :], AF.Sigmoid), [po], [sgo])
            pg = c.ps()
            for kt in range(8):
                c.mm(pg[:Q, 0:8], uT[:, kt, :Q], win[:, kt, 2048:2056], kt == 0, kt == 7, [uT, win], [pg])
            c.op("dve", lambda e, pg=pg: e.tensor_tensor(g8[:Q, :], pg[:Q, 0:8], bgt[:Q, :], ALU.add), [pg, bgt], [g8])
            c.op("act", lambda e: e.activation(lf[:Q, :], g8[:Q, 4:8], AF.Exp, scale=-1.0), [g8], [lf])
            c.op("act", lambda e: e.activation(lf[:Q, :], lf[:Q, :], AF.Ln, bias=1.0, scale=1.0), [lf], [lf])
            c.op("dve", lambda e: e.tensor_scalar(lf[:Q, :], lf[:Q, :], -1.0, None, ALU.mult), [lf], [lf])
            pF = c.ps()
            c.mm(pF[:Q, 0:4], K["maskT"][:Q, :Q], lf[:Q, :], True, True, [K["maskT"], lf], [pF])
            c.op("dve", lambda e, pF=pF: e.tensor_tensor(av[:Q, :], g8[:Q, 0:4], pF[:Q, 0:4], ALU.subtract), [g8, pF], [av])
            c.op("act", lambda e: e.activation(ea[:Q, :], av[:Q, :], AF.Exp), [av], [ea])
            c.op("act", lambda e, pF=pF: e.activation(emF[:Q, :], pF[:Q, 0:4], AF.Exp, scale=-1.0), [pF], [emF])
            for ci in range(nch):
                c.op("dve", lambda e, ci=ci: e.tensor_scalar(lfblk[:Q, ci * 4:(ci + 1) * 4], lf[:Q, :], K["chind"][:Q, ci:ci + 1], None, ALU.mult),
                     [lf, K["chind"]], [lfblk])
            pE = c.ps()
            c.mm(pE[:, 0:4 * nch], K["ones"][:Q, :], lfblk[:Q, 0:4 * nch], True, True, [K["ones"], lfblk], [pE])
            c.op("act", lambda e, pE=pE: e.activation(EF[:, 0:4 * nch], pE[:, 0:4 * nch], AF.Exp), [pE], [EF])
            for h in range(4):
                c.op("dve", lambda e, h=h, pv=pv: e.tensor_scalar(vaug[:Q, h, 0:128], pv[:Q, h * 128:(h + 1) * 128], ea[:Q, h:h + 1], None, ALU.mult), [pv, ea], [vaug])
            c.op("dve", lambda e: e.tensor_copy(vaug[:Q, :, 128:129], ea[:Q, :].unsqueeze(2)), [ea], [vaug])
            for ci in range(nch):
                r0 = ci * T
                pDs = []
                for h in range(4):
                    pD = c.ps()
                    c.mm(pD[:, 0:129], ktm[r0:r0 + T, h * 128:(h + 1) * 128], vaug[r0:r0 + T, h, :], True, True, [ktm, vaug], [pD])
                    pDs.append(pD)
                for h in range(4):
                    c.op("dve", lambda e, h=h, pD=pDs[h]: e.tensor_tensor(tmpS[h][:], pD[:, 0:129], S[h][:], ALU.add), [pDs[h], S[h]], [tmpS[h]])
                for h in range(4):
                    col = ci * 4 + h
                    cbo = Cb[(gc + ci + 1) % 3][h]
                    c.op("dve", lambda e, h=h, col=col: e.tensor_scalar(S[h][:], tmpS[h][:], EF[:, col:col + 1], None, ALU.mult), [tmpS[h], EF], [S[h]])
                    c.op("act", lambda e, h=h, col=col, cbo=cbo: e.activation(cbo[:], tmpS[h][:], AF.Copy, scale=EF[:, col:col + 1]), [tmpS[h], EF], [cbo])
            pNs = []
            for h in range(4):
                pS = c.ps()
                c.mm(pS[:Q, 0:Q], kT[:, h, :Q], qT[:, h, :Q], True, True, [kT, qT], [pS])
                c.op("dve", lambda e, h=h, pS=pS: e.tensor_tensor(wT[h][:Q, :Q], pS[:Q, 0:Q], K["maskT"][:Q, :Q], ALU.mult), [pS, K["maskT"]], [wT[h]])
            for h in range(4):
                pN = c.ps()
                c.mm(pN[:Q, 0:129], wT[h][:Q, :Q], vaug[:Q, h, :], True, False, [wT[h], vaug], [pN])
                for ci in range(nch):
                    r0 = ci * T
                    cbi = Cb[(gc + ci) % 3][h]
                    c.mm(pN[r0:r0 + T, 0:129], qT[:, h, r0:r0 + T], cbi[:], False, ci == nch - 1, [qT, cbi], [pN])
                pNs.append(pN)
            hm = hmlt[k % 2]
            for h in range(4):
                c.op("dve", lambda e, h=h: e.tensor_copy(dn[h][:Q, 0:1], pNs[h][:Q, 128:129]), [pNs[h]], [dn[h]])
            for h in range(4):
                c.op("dve", lambda e, h=h: e.scalar_tensor_tensor(dn[h][:Q, 1:2], dn[h][:Q, 0:1], -1.0, dn[h][:Q, 0:1], ALU.mult, ALU.max), [dn[h]], [dn[h]])
            for h in range(4):
                c.op("dve", lambda e, h=h: e.tensor_tensor(dn[h][:Q, 2:3], dn[h][:Q, 1:2], emF[:Q, h:h + 1], ALU.max), [dn[h], emF], [dn[h]])
            for h in range(4):
                c.op("dve", lambda e, h=h: e.reciprocal(dn[h][:Q, 3:4], dn[h][:Q, 2:3]), [dn[h]], [dn[h]])
            for h in range(4):
                c.op("dve", lambda e, h=h: e.tensor_scalar(hn[h][:Q, :], pNs[h][:Q, 0:128], dn[h][:Q, 3:4], None, ALU.mult), [pNs[h], dn[h]], [hn[h]])
            for h in range(4):
                c.op("dve", lambda e, h=h: e.bn_stats(stt[h][:Q, 0:6], hn[h][:Q, :]), [hn[h]], [stt[h]])
            for h in range(4):
                c.op("dve", lambda e, h=h: e.bn_aggr(stt[h][:Q, 6:8], stt[h][:Q, 0:6]), [stt[h]], [stt[h]])
            for h in range(4):
                c.op("act", lambda e, h=h: e.activation(stt[h][:Q, 8:9], stt[h][:Q, 7:8], AF.Sqrt, bias=K["eps"][:Q, 0:1], scale=1.0), [stt[h], K["eps"]], [stt[h]])
            for h in range(4):
                c.op("dve", lambda e, h=h: e.reciprocal(stt[h][:Q, 9:10], stt[h][:Q, 8:9]), [stt[h]], [stt[h]])
            for h in range(4):
                c.op("dve", lambda e, h=h: e.tensor_scalar(hn[h][:Q, :], hn[h][:Q, :], stt[h][:Q, 6:7], stt[h][:Q, 9:10], ALU.subtract, ALU.mult), [hn[h], stt[h]], [hn[h]])
            for h in range(4):
                c.op("dve", lambda e, h=h: e.tensor_tensor(hn[h][:Q, :], hn[h][:Q, :], gml[:Q, h * 128:(h + 1) * 128], ALU.mult), [hn[h], gml], [hn[h]])
            for h in range(4):
                c.op("dve", lambda e, h=h, hm=hm: e.tensor_tensor(hm[:Q, h * 128:(h + 1) * 128], hn[h][:Q, :], sgo[:Q, h * 128:(h + 1) * 128], ALU.mult), [hn[h], sgo], [hm])
            c.dma("pool", D["hml"][row0:row0 + Q, :], hm[:Q, :], reads=[hm], owner=hm)
            gc += nch
            last = (k + 1 == len(tiles)) or (tiles[k + 1][2] != sidx)
            if last:
                bcast_m(-1.0)
                for h in range(4):
                    c.op("dve", lambda e, h=h: e.tensor_scalar(cst[:, h, :], S[h][:], EM[:, h:h + 1], None, ALU.mult), [S[h], EM], [cst])
                p = c.ps()
                c.tr(p[0:4, 0:128], cst[:, :, 128], K["ident"][:, :], [cst, K["ident"]], [p])
                c.op("dve", lambda e, p=p: e.tensor_copy(nst[:, :], p[0:4, 0:128]), [p], [nst])
                if sidx == 0:
                    Cd, nd, md = D["C_p"][l], D["n_p"][l], D["m_p"][l]
                else:
                    Cd, nd, md = D["C_s"][l, sidx - 1], D["n_s"][l, sidx - 1], D["m_s"][l, sidx - 1]
                c.dma("pool", Cd.rearrange("h k v -> k h v"), cst[:, :, 0:128], reads=[cst], owner=cst)
                c.dma("pool", nd, nst[:, :], reads=[nst], owner=nst)
                c.dma("pool", md.rearrange("(h o) -> h o", o=1), mprev[:, 0:1], reads=[mprev], owner=mprev)
        c.barrier()


def phase_b(c, nc, D, K, l, SEQ, NSEQ, ptiles, stiles, load_mod, layer_norm, dbg):
    with ExitStack() as ph:
        wup = c.sb(ph, "wup", [128, 8, 4096], BF16)
        wdn = c.sb(ph, "wdn", [128, 32, 1024], BF16)
        for kt in range(8):
            c.dma("sp", wup[:, kt, :], D["w_up_b"][l, kt * 128:(kt + 1) * 128, :], writes=[wup], owner=wup, partial=True)
        for fc in range(32):
            c.dma("sp", wdn[:, fc, :], D["w_down_b"][l, fc * 128:(fc + 1) * 128, :], writes=[wdn], owner=wdn, partial=True)
        MOD = [c.sb(ph, "modb%d" % i, [128, 1024]) for i in range(3)]
        lng = c.sb(ph, "lngb", [128, 1024]); lnb = c.sb(ph, "lnbb", [128, 1024])
        c.dma("sp", lng[:], D["ln_g"][l, 1:2, :].partition_broadcast(128), writes=[lng], owner=lng)
        c.dma("sp", lnb[:], D["ln_b"][l, 1:2, :].partition_broadcast(128), writes=[lnb], owner=lnb)
        xm = [c.sb(ph, "xm%d" % i, [128, 1024]) for i in range(2)]
        u2 = c.sb(ph, "u2", [128, 1024], BF16); u2T = c.sb(ph, "u2T", [128, 8, 128], BF16)
        hT = c.sb(ph, "hT", [128, 32, 128], BF16)
        rl = [c.sb(ph, "rl%d" % i, [128, 512]) for i in range(2)]
        pre = c.sb(ph, "preb", [128, 1024])
        xo = [c.sb(ph, "xo%d" % i, [128, 1024]) for i in range(2)]
        st = c.sb(ph, "stb", [128, 16])
        tiles = ptiles + stiles

        def load_x(k):
            row0, Q, sidx, ti = tiles[k]
            c.dma("sp", xm[k % 2][:Q, :], D["xmid"][row0:row0 + Q, :], writes=[xm[k % 2]], owner=xm[k % 2])

        load_x(0)
        cur_seq = None
        for k, (row0, Q, sidx, ti) in enumerate(tiles):
            if k + 1 < len(tiles):
                load_x(k + 1)
            if sidx != cur_seq:
                cur_seq = sidx
                for i, sec in enumerate([3, 4, 5]):
                    load_mod(MOD[i], l, sec, sidx, Q)
            x = xm[k % 2]
            make_uT(c, K, x, Q, MOD[0], MOD[1], u2, u2T)
            for g4 in range(8):
                p = c.ps()
                for j in range(4):
                    fc = g4 * 4 + j
                    for kt in range(8):
                        c.mm(p[:, j * Q:(j + 1) * Q], wup[:, kt, fc * 128:(fc + 1) * 128], u2T[:, kt, :Q], kt == 0, kt == 7, [wup, u2T], [p])
                r = rl[g4 % 2]
                c.op("act", lambda e, r=r, p=p: e.activation(r[:, 0:4 * Q], p[:, 0:4 * Q], AF.Relu), [p], [r])
                c.op("pool", lambda e, r=r, g4=g4: e.tensor_tensor(hT[:, g4 * 4:(g4 + 1) * 4, :Q], r[:, 0:4 * Q].rearrange("p (j q) -> p j q", q=Q),
                                                              r[:, 0:4 * Q].rearrange("p (j q) -> p j q", q=Q), ALU.mult), [r], [hT])
            pA = c.ps(); pB = c.ps()
            for fc in range(32):
                c.mm(pA[:Q, :], hT[:, fc, :Q], wdn[:, fc, 0:512], fc == 0, fc == 31, [hT, wdn], [pA])
                c.mm(pB[:Q, :], hT[:, fc, :Q], wdn[:, fc, 512:1024], fc == 0, fc == 31, [hT, wdn], [pB])
            for hh, pp in enumerate([pA, pB]):
                c.op("dve", lambda e, hh=hh, pp=pp: e.tensor_tensor(pre[:Q, hh * 512:(hh + 1) * 512], pp[:Q, :], MOD[2][:Q, hh * 512:(hh + 1) * 512], ALU.mult), [pp, MOD[2]], [pre])
            c.op("dve", lambda e: e.scalar_tensor_tensor(pre[:Q, :], x[:Q, :], ALPHA, pre[:Q, :], ALU.mult, ALU.add), [x, pre], [pre])
            o = xo[k % 2]
            layer_norm(st, pre, Q, lng, lnb, o, o)
            if l == 0:
                dst = D["x1"][row0:row0 + Q, :]
            elif sidx == 0:
                dst = D["yp"][row0:row0 + Q, :]
            else:
                dst = D["ys"][row0 - SEQ:row0 - SEQ + Q, :]
            c.dma("pool", dst, o[:Q, :], reads=[o], owner=o)
        c.barrier()


def _in_maps(inp, NSEQ):
    f = lambda a: np.ascontiguousarray(np.asarray(a))
    rel = f(inp["rel_bias"]).astype(np.float32)
    consts = host_consts()
    consts.update(bias_tiles(rel))
    w_in = f(inp["w_in"])
    q0 = 2056
    perm = np.arange(N_IN)
    blk = []
    for r in range(4):
        blk.extend(range(q0 + r * 64, q0 + (r + 1) * 64))
        blk.extend(range(q0 + (4 + r) * 64, q0 + (5 + r) * 64))
    perm[q0:q0 + 512] = np.array(blk)
    w_in_p = np.ascontiguousarray(w_in[:, :, perm])
    NPHYS = inp["cache_cmp_kv"].shape[1]
    pool_c = f(inp["cache_cmp_kv"]).reshape(2, NPHYS * 128, 256)
    pool_s = f(inp["cache_slc_kv"]).reshape(2, NPHYS * 128, 256)
    shared = {
        "pool_c": pool_c, "pool_s": pool_s,
        "w_ada": f(inp["w_ada"]), "b_ada": f(inp["b_ada"]), "w_in": w_in_p, "b_gate": f(inp["b_gate"]),
        "ml_g": f(inp["ml_norm_g"]), "cmp_pe": f(inp["cmp_pe"]).reshape(2, 2, 2048),
        "cmp_w1": f(inp["cmp_w1"]), "cmp_w2": f(inp["cmp_w2"]), "w_out": f(inp["w_out"]),
        "ln_g": f(inp["ln_g"]), "ln_b": f(inp["ln_b"]), "w_up": f(inp["w_up"]), "w_down": f(inp["w_down"]),
    }
    shared.update(consts)
    maps = []
    for core in range(N_CORES):
        b = core % 2
        sl = slice(core * NSEQ, (core + 1) * NSEQ)
        m = dict(shared)
        m["xp"] = f(inp["x_prompt"][b])
        m["xs"] = f(inp["x_sample"][sl]).reshape(NSEQ * 4, 1024)
        m["win_c"] = f(inp["cache_win_kv"][:, sl]).reshape(2, NSEQ, 512, 256)
        m["stC"] = f(inp["state_mlstm_C"][:, sl])
        m["stn"] = f(inp["state_mlstm_n"][:, sl])
        m["stm"] = f(inp["state_mlstm_m"][:, sl])
        m["ptab"] = f(inp["page_table"][sl]).astype(np.int32)
        m["c_all"] = np.concatenate([f(inp["c_prompt"][b:b + 1]), f(inp["c_sample"][sl])], axis=0)
        maps.append(m)
    return maps, NPHYS


def run_cores(inp, dbg=False, stop_after=None):
    SEQ = inp["x_prompt"].shape[1]
    NSEQ = inp["x_sample"].shape[0] // N_CORES
    maps, NPHYS = _in_maps(inp, NSEQ)
    nc = build(SEQ, NSEQ, NPHYS, dbg=dbg, stop_after=stop_after)
    res = run_bass_kernel_spmd(nc, maps, core_ids=list(range(N_CORES)))
    return res.results, SEQ, NSEQ


def kernel(**inp):
    R, SEQ, NSEQ = run_cores(inp)
    B = 2
    cat = lambda name: np.concatenate([R[c][name] for c in range(N_CORES)], axis=0)
    catl = lambda name: np.concatenate([R[c][name] for c in range(N_CORES)], axis=1)
    yp = np.stack([R[b]["yp"] for b in range(B)])
    ys = cat("ys").reshape(N_CORES * NSEQ, 4, 1024)
    rows_p = lambda nm: np.stack([R[b][nm] for b in range(B)], axis=1).reshape(2, B, -1, 2, 2, 64)
    rows_s = lambda nm: catl(nm).reshape(2, N_CORES * NSEQ, 4, 2, 2, 64)
    win_s = catl("win_s").reshape(2, N_CORES * NSEQ, 512, 2, 2, 64)
    C_p = np.stack([R[b]["C_p"] for b in range(B)], axis=1)
    n_p = np.stack([R[b]["n_p"] for b in range(B)], axis=1)
    m_p = np.stack([R[b]["m_p"] for b in range(B)], axis=1)
    outs = (yp, ys, rows_p("cmp_p"), rows_s("cmp_s"), rows_p("slc_p"), rows_s("slc_s"), rows_p("win_p"), win_s,
            C_p, catl("C_s"), n_p, catl("n_s"), m_p, catl("m_s"))
    return tuple(np.ascontiguousarray(o).astype(np.float32) for o in outs)


def phase_a2(c, nc, D, K, l, SEQ, NSEQ, NPHYS, ptiles, stiles, load_mod, layer_norm, dbg):
    NT = SEQ // 128
    KW = max(SEQ, 17 * 128)
    NKT = KW // 128
    QS = 0.125
    GC = 1.5957691216057308
    with ExitStack() as ph:
        win = c.sb(ph, "win2", [128, 8, 1304], BF16)
        for kt in range(8):
            c.dma("sp", win[:, kt, :], D["w_in_b"][l, kt * 128:(kt + 1) * 128, 2056:3360], writes=[win], owner=win, partial=True)
        wout = c.sb(ph, "wout", [128, 8, 1024], BF16)
        for kt in range(8):
            c.dma("sp", wout[:, kt, :], D["w_out_b"][l, kt * 128:(kt + 1) * 128, :], writes=[wout], owner=wout, partial=True)
        W1 = c.sb(ph, "W1", [128, 2, 16, 256], BF16)
        for s in range(2):
            c.dma("sp", W1[:, s, :, :], D["w1_b"][l, s].rearrange("(j p) h -> p j h", p=128), writes=[W1], owner=W1, partial=True)
        W2k = c.sb(ph, "W2k", [128, 2, 2, 128], BF16)
        W2v = c.sb(ph, "W2v", [128, 2, 64], BF16)
        c.op("pool", lambda e: e.memset(W2k[:], 0.0), [], [W2k])
        for g in range(2):
            c.dma("sp", W2k[:, :, g, 64 * g:64 * g + 64], D["w2_b"][l, 0].rearrange("(hc h) d -> h hc d", h=128), writes=[W2k], owner=W2k, partial=(g == 1))
        c.dma("sp", W2v[:, :, :], D["w2_b"][l, 1].rearrange("(hc h) d -> h hc d", h=128), writes=[W2v], owner=W2v)
        pet = c.sb(ph, "pet", [128, 2, 16], BF16)
        for s in range(2):
            c.dma("sp", None, None, writes=[pet], owner=pet, partial=(s == 1),
                  fn=lambda e, s=s: e.dma_start(out=pet[:, s, :], in_=D["pe_b"][l, s].rearrange("(j p) -> p j", p=128), allow_slow_non_contiguous=True))
        b1 = c.sb(ph, "b1c", [128, 4])
        p = c.ps()
        for s in range(2):
            for hc in range(2):
                col = s * 2 + hc
                for j in range(16):
                    c.mm(p[:, col:col + 1], W1[:, s, j, hc * 128:(hc + 1) * 128], pet[:, s, j:j + 1], j == 0, j == 15, [W1, pet], [p])
        c.op("dve", lambda e, p=p: e.tensor_copy(b1[:], p[:, 0:4]), [p], [b1])
        MODsh = c.sb(ph, "a2sh", [128, 1024]); MODsc = c.sb(ph, "a2sc", [128, 1024]); MODg = c.sb(ph, "a2g", [128, 1024])
        lng = c.sb(ph, "lnga", [128, 1024]); lnb = c.sb(ph, "lnba", [128, 1024])
        c.dma("sp", lng[:], D["ln_g"][l, 0:1, :].partition_broadcast(128), writes=[lng], owner=lng)
        c.dma("sp", lnb[:], D["ln_b"][l, 0:1, :].partition_broadcast(128), writes=[lnb], owner=lnb)
        slcK = c.sb(ph, "slcK", [128, KW], BF16); slcV = c.sb(ph, "slcV", [128, NKT, 128], BF16)
        winK = c.sb(ph, "winK", [128, 8, 128], BF16); winV = c.sb(ph, "winV", [128, 8, 128], BF16)
        X = [[c.sb(ph, "X%d%d" % (s, w), [128, 144], BF16) for w in range(2)] for s in range(2)]
        hidv = c.sb(ph, "hidv", [128, 2, 2, 512], BF16)
        kcT = c.sb(ph, "kcT", [128, 512], BF16); vc = c.sb(ph, "vc", [128, 4, 128], BF16)
        for t in [slcK, slcV, winK, winV, hidv, kcT, vc, X[0][0], X[0][1], X[1][0], X[1][1]]:
            c.op("pool", lambda e, t=t: e.memset(t[:], 0.0), [], [t])
        xt = c.sb(ph, "a2x", [128, 1024]); ub = c.sb(ph, "a2ub", [128, 1024], BF16); uT = c.sb(ph, "a2uT", [128, 8, 128], BF16)
        qTn = c.sb(ph, "qZ", [128, 8, 128], BF16)
        c.op("pool", lambda e: e.memset(qTn[:], 0.0), [], [qTn])
        kvr = c.sb(ph, "kvr", [128, 768]); gts = c.sb(ph, "gts", [128, 24])
        hx = c.sb(ph, "hx", [128, 64]); hy = c.sb(ph, "hy", [128, 64]); hidb = c.sb(ph, "hidb", [128, 64], BF16)
        sc = c.sb(ph, "scrow", [128, KW])
        scR = [Res("scR%d" % k, sc.t) for k in range((KW + 511) // 512)]
        pcf = c.sb(ph, "pcf", [128, 512])
        PT = [c.sb(ph, "PT%d" % i, [128, 4, 128], BF16) for i in range(2)]
        oacc = c.sb(ph, "oacc", [128, 512]); oaccR = [Res("oaccR%d" % h, oacc.t) for h in range(8)]; mixed = c.sb(ph, "mixed", [128, 1024], BF16); mT = c.sb(ph, "mixT", [128, 8, 128], BF16)
        pre = c.sb(ph, "prea", [128, 1024]); xo = c.sb(ph, "xmo", [128, 1024]); st = c.sb(ph, "sta", [128, 16])
        Pg = [c.sb(ph, "Pg%d" % g, [128, 512]) for g in range(2)]
        A = [c.sb(ph, "A%d" % g, [128, 128]) for g in range(2)]
        imp = c.sb(ph, "imp", [128, 128]); cand = c.sb(ph, "cand", [128, 128]); scs = c.sb(ph, "scs", [128, 128]); wk8 = c.sb(ph, "wk8", [128, 128])
        frc = c.sb(ph, "frc", [128, 128]); m8 = c.sb(ph, "m8", [128, 16])
        sm = [c.sb(ph, "sm%d" % h, [128, 8]) for h in range(8)]
        c.op("pool", lambda e: e.memset(imp[:], 0.0), [], [imp])
        gbuf = [c.sb(ph, "gbuf%d" % i, [128, 256]) for i in range(2)]
        gb16 = c.sb(ph, "gb16", [128, 2, 256], BF16)
        idxf = c.sb(ph, "idxf", [128, 16]); idxi = c.sb(ph, "idxi", [128, 16], I32); pidx = c.sb(ph, "pidx", [128, 1])
        c.op("pool", lambda e: e.iota(pidx[:], [[0, 1]], base=0, channel_multiplier=1, allow_small_or_imprecise_dtypes=True), [], [pidx])
        if os.environ.get("SBREM"):
            print("A2 sbuf remaining", nc.sbuf_bytes_remaining)

        def compress_tile(i):
            j0 = 1 if i == 0 else 0
            n0 = 8 * i - 1 + j0
            nj = 8 - j0
            p = c.ps()
            for s in range(2):
                for hc in range(2):
                    for g in range(2):
                        reg = ((s * 2 + hc) * 2 + g) * 8
                        Xs = X[s][g]
                        for jj in range(16):
                            c.mm(p[:, reg + j0:reg + 8], W1[:, s, jj, hc * 128:(hc + 1) * 128],
                                 Xs[:, 2 * jj + 16 * j0:2 * jj + 16 * 7 + 1:16], jj == 0, jj == 15, [W1, Xs], [p])
            CL = int(os.environ.get("CL", "9"))
            if CL < 2:
                return
            for s in range(2):
                for hc in range(2):
                    col = s * 2 + hc
                    c.op("act", lambda e, p=p, col=col: e.activation(hx[:, col * 16:(col + 1) * 16], p[:, col * 16:(col + 1) * 16], AF.Identity, bias=b1[:, col:col + 1], scale=1.0), [p, b1], [hx])
            if CL < 3:
                return
            c.op("dve", lambda e: e.tensor_tensor(hy[:], hx[:], hx[:], ALU.mult), [hx], [hy])
            c.op("dve", lambda e: e.tensor_scalar(hy[:], hy[:], 0.044715, 1.0, ALU.mult, ALU.add), [hy], [hy])
            c.op("dve", lambda e: e.tensor_tensor(hy[:], hy[:], hx[:], ALU.mult), [hy, hx], [hy])
            c.op("act", lambda e: e.activation(hy[:], hy[:], AF.Sigmoid, scale=GC), [hy], [hy])
            c.op("dve", lambda e: e.tensor_tensor(hidb[:], hy[:], hx[:], ALU.mult), [hy, hx], [hidb])
            if CL < 4:
                return
            pk = c.ps()
            first = True
            for hc in range(2):
                for g in range(2):
                    reg = ((0 * 2 + hc) * 2 + g) * 8
                    c.mm(pk[:, 0:nj], W2k[:, hc, g, :], hidb[:, reg + j0:reg + 8], first, (hc == 1 and g == 1), [W2k, hidb], [pk])
                    first = False
            c.op("act", lambda e, pk=pk: e.activation(kcT[:, n0:n0 + nj], pk[:, 0:nj], AF.Copy), [pk], [kcT])
            if CL < 5:
                return
            c.op("dve", lambda e: e.tensor_copy(hidv[:, :, :, n0:n0 + nj].rearrange("p a b n -> p (a b) n"),
                                                hidb[:, 32:64].rearrange("p (a n) -> p a n", n=8)[:, :, j0:8]), [hidb], [hidv])
            if CL < 6:
                return
            for jt in sorted(set([n0 // 128, (n0 + nj - 1) // 128])):
                pv = c.ps()
                for g in range(2):
                    for hc in range(2):
                        c.mm(pv[:, 64 * g:64 * g + 64], hidv[:, hc, g, jt * 128:(jt + 1) * 128], W2v[:, hc, :], hc == 0, hc == 1, [hidv, W2v], [pv])
                c.op("act", lambda e, pv=pv, jt=jt: e.activation(vc[:, jt, :], pv[:, 0:128], AF.Copy), [pv], [vc])

        def carry_X():
            for s in range(2):
                for w in range(2):
                    t = X[s][w]
                    c.op("pool", lambda e, t=t: e.tensor_copy(t[:, 0:16], t[:, 128:144]), [t], [t])

        def softmax_pv(h, g, Q, src, nkeys, vfn, gate_col, first_branch, rowmask=None):
            srcap, srcres, dst, dstres = src
            srcl = srcres if isinstance(srcres, list) else [srcres]
            dstl = dstres if isinstance(dstres, list) else [dstres]
            res_kt = (lambda kt: dstl[kt // 4]) if isinstance(dstres, list) else (lambda kt: dstres)
            s_ = sm[h]
            c.op("dve", lambda e: e.tensor_reduce(s_[:Q, 0:1], srcap, AX.X, ALU.max, negate=True), srcl, [s_])
            c.op("act", lambda e: e.activation(dst, srcap, AF.Exp, bias=s_[:Q, 0:1], scale=1.0, accum_out=s_[:Q, 1:2]), srcl + [s_], dstl + [s_])
            c.op("dve", lambda e: e.reciprocal(s_[:Q, 2:3], s_[:Q, 1:2]), [s_], [s_])
            c.op("dve", lambda e: e.tensor_tensor(s_[:Q, 3:4], s_[:Q, 2:3], gts[:Q, gate_col:gate_col + 1], ALU.mult), [s_, gts], [s_])
            if rowmask is not None:
                c.op("dve", lambda e: e.tensor_tensor(s_[:Q, 3:4], s_[:Q, 3:4], rowmask[:Q, 0:1], ALU.mult), [s_, rowmask], [s_])
            nkt = (nkeys + 127) // 128
            pO = c.ps_acc()
            for k0 in range(0, nkt, 4):
                kn = min(4, nkt - k0)
                pT = c.ps()
                ws = []
                for j in range(kn):
                    kt = k0 + j
                    w = min(128, nkeys - kt * 128)
                    ws.append(w)
                    c.tr(pT[:w, j * Q:(j + 1) * Q], dst[:, kt * 128:kt * 128 + w], K["ident"][:Q, :Q], [res_kt(kt), K["ident"]], [pT])
                pt = PT[(k0 // 4) % 2]
                if min(ws) == 128:
                    eng = "act" if (k0 // 4) % 2 == 0 else "dve"
                    if eng == "act":
                        c.op("act", lambda e, pT=pT, pt=pt, kn=kn: e.activation(pt[:, 0:kn, :Q], pT[:, 0:kn * Q].rearrange("p (k q) -> p k q", q=Q), AF.Copy), [pT], [pt])
                    else:
                        c.op("dve", lambda e, pT=pT, pt=pt, kn=kn: e.tensor_scalar(pt[:, 0:kn, :Q], pT[:, 0:kn * Q].rearrange("p (k q) -> p k q", q=Q), 1.0, None, ALU.mult), [pT], [pt])
                else:
                    for j in range(kn):
                        c.op("dve", lambda e, pT=pT, pt=pt, j=j, w=ws[j]: e.tensor_scalar(pt[:w, j, :Q], pT[:w, j * Q:(j + 1) * Q], 1.0, None, ALU.mult), [pT], [pt])
                for j in range(kn):
                    kt = k0 + j
                    vap, vres = vfn(kt, ws[j])
                    c.mm(pO[:Q, 0:64], pt[:ws[j], j, :Q], vap, kt == 0, kt == nkt - 1, [pt, vres], [pO])
            oc = oacc[:Q, h * 64:(h + 1) * 64]
            if first_branch:
                c.op("dve", lambda e: e.tensor_scalar(oc, pO[:Q, 0:64], s_[:Q, 3:4], None, ALU.mult), [pO, s_], [oaccR[h]])
            else:
                c.op("dve", lambda e: e.scalar_tensor_tensor(oc, pO[:Q, 0:64], s_[:Q, 3:4], oc, ALU.mult, ALU.add), [pO, s_, oaccR[h]], [oaccR[h]])

        def attend(i, Q, nk, bct, jmax, Jt, use_sel):
            if use_sel:
                c.op("dve", lambda e: e.tensor_scalar(frc[:Q, :], Jt[:Q, :], float(jmax - 1), None, ALU.is_equal), [Jt], [frc])
                c.op("dve", lambda e: e.scalar_tensor_tensor(frc[:Q, :], Jt[:Q, :], float(jmax), frc[:Q, :], ALU.is_equal, ALU.max), [Jt, frc], [frc])
                c.op("dve", lambda e: e.tensor_tensor(frc[:Q, :], frc[:Q, :], K["z0"][:Q, :], ALU.max), [frc, K["z0"]], [frc])
            for g in range(2):
                for r in range(4):
                    h = 4 * g + r
                    qT = qTn[:, h, :Q]
                    pC = c.ps()
                    c.mm(pC[:Q, 0:nk], qT, kcT[:, 0:nk], True, True, [qTn, kcT], [pC])
                    w = min(16, nk)
                    c.op("dve", lambda e, pC=pC, w=w, h=h: e.tensor_tensor(pC[:Q, nk - w:nk], pC[:Q, nk - w:nk], bct[:Q, h, 16 - w:16], ALU.add), [pC, bct], [pC])
                    softmax_pv(h, g, Q, (pC[:Q, 0:nk], pC, pcf[:Q, 0:nk], pcf), nk, lambda kt, w_, g=g: (vc[:w_, kt, 64 * g:64 * g + 64], vc), h * 3 + 0, True,
                               rowmask=(K["rv0"] if (i == 0 and Q == 128) else None))
                    s_ = sm[h]
                    if use_sel:
                        if r == 0:
                            c.op("dve", lambda e, g=g, s_=s_: e.tensor_scalar(Pg[g][:Q, 0:nk], pcf[:Q, 0:nk], s_[:Q, 2:3], None, ALU.mult), [pcf, s_], [Pg[g]])
                        else:
                            c.op("dve", lambda e, g=g, s_=s_: e.scalar_tensor_tensor(Pg[g][:Q, 0:nk], pcf[:Q, 0:nk], s_[:Q, 2:3], Pg[g][:Q, 0:nk], ALU.mult, ALU.add), [pcf, s_, Pg[g]], [Pg[g]])
                if use_sel:
                    P_ = Pg[g]
                    c.op("dve", lambda e, P_=P_: e.tensor_reduce(imp[:Q, 0:jmax], P_[:Q, 0:4 * jmax].rearrange("q (j f) -> q j f", f=4), AX.X, ALU.add), [P_], [imp])
                    c.op("dve", lambda e, P_=P_: e.tensor_tensor(imp[:Q, 1:jmax], imp[:Q, 1:jmax], P_[:Q, 3:4 * jmax - 4:4], ALU.add), [imp, P_], [imp])
                    c.op("dve", lambda e: e.tensor_scalar(cand[:Q, :], Jt[:Q, :], float(jmax - 2), None, ALU.is_le), [Jt], [cand])
                    c.op("dve", lambda e: e.tensor_tensor(cand[:Q, :], cand[:Q, :], K["nz"][:Q, :], ALU.mult), [cand, K["nz"]], [cand])
                    c.op("dve", lambda e: e.scalar_tensor_tensor(scs[:Q, :], imp[:Q, :], 1.0, cand[:Q, :], ALU.add, ALU.mult), [imp, cand], [scs])
                    c.op("dve", lambda e: e.max(m8[:Q, 0:8], scs[:Q, :]), [scs], [m8])
                    c.op("dve", lambda e: e.match_replace(wk8[:Q, :], m8[:Q, 0:8], scs[:Q, :], 0.0), [scs, m8], [wk8])
                    c.op("dve", lambda e: e.max(m8[:Q, 8:16], wk8[:Q, :]), [wk8], [m8])
                    c.op("dve", lambda e: e.tensor_scalar(cand[:Q, :], scs[:Q, :], m8[:Q, 12:13], None, ALU.is_ge), [scs, m8], [cand])
                    c.op("dve", lambda e: e.tensor_tensor(cand[:Q, :], cand[:Q, :], frc[:Q, :], ALU.max), [cand, frc], [cand])
                    c.op("dve", lambda e, g=g: e.tensor_scalar(A[g][:Q, :], cand[:Q, :], -1.0, -NEG, ALU.add, ALU.mult), [cand], [A[g]])
                for r in range(4):
                    h = 4 * g + r
                    qT = qTn[:, h, :Q]
                    for c0 in range(0, i + 1, 4):
                        kn = min(4, i + 1 - c0)
                        pS = c.ps()
                        c.mm(pS[:Q, 0:kn * 128], qT, slcK[:, c0 * 128:(c0 + kn) * 128], True, True, [qTn, slcK], [pS])
                        if use_sel:
                            c.op("dve", lambda e, pS=pS, c0=c0, kn=kn, g=g: e.tensor_tensor(
                                sc[:Q, c0 * 128:(c0 + kn) * 128].rearrange("q (b k) -> q b k", k=64),
                                pS[:Q, 0:kn * 128].rearrange("q (b k) -> q b k", k=64),
                                A[g][:Q, 2 * c0:2 * (c0 + kn)].unsqueeze(2).to_broadcast([Q, 2 * kn, 64]), ALU.add), [pS, A[g]], [scR[c0 // 4]])
                        else:
                            c.op("act", lambda e, pS=pS, c0=c0, kn=kn: e.activation(sc[:Q, c0 * 128:(c0 + kn) * 128], pS[:Q, 0:kn * 128], AF.Copy), [pS], [scR[c0 // 4]])
                    c.op("dve", lambda e, h=h: e.tensor_tensor(sc[:Q, i * 128:(i + 1) * 128], sc[:Q, i * 128:(i + 1) * 128], K["b0"][:Q, h, :], ALU.add), [scR[i // 4], K["b0"]], [scR[i // 4]])
                    if i >= 1:
                        c.op("dve", lambda e, h=h: e.tensor_tensor(sc[:Q, (i - 1) * 128:i * 128], sc[:Q, (i - 1) * 128:i * 128], K["b1"][:Q, h, :], ALU.add), [scR[(i - 1) // 4], K["b1"]], [scR[(i - 1) // 4]])
                    nkeys = (i + 1) * 128
                    softmax_pv(h, g, Q, (sc[:Q, 0:nkeys], scR[0:(nkeys + 511) // 512], sc[:Q, 0:nkeys], scR[0:(nkeys + 511) // 512]), nkeys, lambda kt, w_, g=g: (slcV[:w_, kt, 64 * g:64 * g + 64], slcV), h * 3 + 1, False)
                    lo = max(0, i - 4)
                    nw = i - lo + 1
                    pW = [c.ps(), c.ps()]
                    for j in range(nw):
                        kt = lo + j
                        pw = pW[j // 4]
                        c.mm(pw[:Q, (j % 4) * 128:(j % 4 + 1) * 128], qT, winK[:, kt % 8, :], True, True, [qTn, winK], [pw])
                    for j in range(nw):
                        kt = lo + j
                        d = i - kt
                        pw = pW[j // 4]
                        src = pw[:Q, (j % 4) * 128:(j % 4 + 1) * 128]
                        dst = sc[:Q, j * 128:(j + 1) * 128]
                        bt = {0: K["b0"], 1: K["b1"], 4: K["m4"]}.get(d)
                        if bt is None:
                            c.op("act", lambda e, src=src, dst=dst: e.activation(dst, src, AF.Copy), [pw], [scR[j // 4]])
                        elif d == 4:
                            c.op("dve", lambda e, src=src, dst=dst: e.tensor_tensor(dst, src, K["m4"][:Q, :], ALU.add), [pw, K["m4"]], [scR[j // 4]])
                        else:
                            c.op("dve", lambda e, src=src, dst=dst, bt=bt, h=h: e.tensor_tensor(dst, src, bt[:Q, h, :], ALU.add), [pw, bt], [scR[j // 4]])
                    softmax_pv(h, g, Q, (sc[:Q, 0:nw * 128], scR[0:(nw * 128 + 511) // 512], sc[:Q, 0:nw * 128], scR[0:(nw * 128 + 511) // 512]), nw * 128,
                               lambda kt, w_, g=g, lo=lo: (winV[:w_, (lo + kt) % 8, 64 * g:64 * g + 64], winV), h * 3 + 2, False)

        def finish_tile(row0, Q, x):
            c.dma("sp", mixed[:Q, 0:512], D["hml"][row0:row0 + Q, :], writes=[mixed], owner=mixed)
            c.op("act", lambda e: e.activation(mixed[:Q, 512:1024], oacc[:Q, :], AF.Copy), oaccR, [mixed])
            if dbg:
                c.dma("pool", D["onsa"][row0:row0 + Q, :], mixed[:Q, 512:1024], reads=[mixed], owner=oacc)
            p = c.ps()
            pb = p[:, :].bitcast(BF16)
            for kt in range(8):
                c.tr(pb[:, kt * Q:(kt + 1) * Q], mixed[:Q, kt * 128:(kt + 1) * 128], K["identb"][:Q, :Q], [mixed, K["identb"]], [p])
            c.op("act", lambda e, pb=pb: e.activation(mT[:, :, :Q], pb[:, 0:8 * Q].rearrange("p (k q) -> p k q", q=Q), AF.Copy), [p], [mT])
            pA = c.ps(); pB = c.ps()
            for kt in range(8):
                c.mm(pA[:Q, :], mT[:, kt, :Q], wout[:, kt, 0:512], kt == 0, kt == 7, [mT, wout], [pA])
                c.mm(pB[:Q, :], mT[:, kt, :Q], wout[:, kt, 512:1024], kt == 0, kt == 7, [mT, wout], [pB])
            for hh, pp in enumerate([pA, pB]):
                c.op("dve", lambda e, hh=hh, pp=pp: e.tensor_tensor(pre[:Q, hh * 512:(hh + 1) * 512], pp[:Q, :], MODg[:Q, hh * 512:(hh + 1) * 512], ALU.mult), [pp, MODg], [pre])
            c.op("dve", lambda e: e.scalar_tensor_tensor(pre[:Q, :], x[:Q, :], ALPHA, pre[:Q, :], ALU.mult, ALU.add), [x, pre], [pre])
            layer_norm(st, pre, Q, lng, lnb, xo, xo)
            c.dma("pool", D["xmid"][row0:row0 + Q, :], xo[:Q, :], reads=[xo], owner=xo)

        def project(i, Q, x, row_outs):
            make_uT(c, K, x, Q, MODsh, MODsc, ub, uT)
            p = c.ps()
            for r in range(4):
                for kt in range(8):
                    c.mm(p[:, r * Q:(r + 1) * Q], win[:, kt, r * 128:(r + 1) * 128], uT[:, kt, :Q], kt == 0, kt == 7, [win, uT], [p])
            c.op("act", lambda e, p=p: e.activation(qTn[0:64, 0:4, :Q], p[0:64, 0:4 * Q].rearrange("p (r q) -> p r q", q=Q), AF.Copy, scale=QS), [p], [qTn])
            c.op("act", lambda e, p=p: e.activation(qTn[64:128, 4:8, :Q], p[64:128, 0:4 * Q].rearrange("p (r q) -> p r q", q=Q), AF.Copy, scale=QS), [p], [qTn])
            for col0, dstap, dres in [(768, slcK[:, i * 128:i * 128 + Q], slcK), (1024, winK[:, i % 8, :Q], winK)]:
                p = c.ps()
                for kt in range(8):
                    c.mm(p[:, 0:Q], win[:, kt, col0:col0 + 128], uT[:, kt, :Q], kt == 0, kt == 7, [win, uT], [p])
                c.op("act", lambda e, p=p, dstap=dstap: e.activation(dstap, p[:, 0:Q], AF.Copy), [p], [dres])
            pa = c.ps(); pb_ = c.ps()
            for kt in range(8):
                c.mm(pa[:Q, :], uT[:, kt, :Q], win[:, kt, 512:1024], kt == 0, kt == 7, [uT, win], [pa])
            for kt in range(8):
                c.mm(pb_[:Q, 0:280], uT[:, kt, :Q], win[:, kt, 1024:1304], kt == 0, kt == 7, [uT, win], [pb_])
            c.op("act", lambda e: e.activation(kvr[:Q, 0:512], pa[:Q, :], AF.Copy), [pa], [kvr])
            c.op("dve", lambda e: e.tensor_scalar(kvr[:Q, 512:768], pb_[:Q, 0:256], 1.0, None, ALU.mult), [pb_], [kvr])
            c.op("act", lambda e: e.activation(gts[:Q, :], pb_[:Q, 256:280], AF.Sigmoid), [pb_], [gts])
            c.op("dve", lambda e: e.tensor_copy(slcV[:Q, i, :], kvr[:Q, 384:512]), [kvr], [slcV])
            c.op("dve", lambda e: e.tensor_copy(winV[:Q, i % 8, :], kvr[:Q, 640:768]), [kvr], [winV])
            for dstap, c0 in row_outs:
                c.dma("pool", dstap, kvr[:Q, c0:c0 + 256], reads=[kvr], owner=kvr)

        def project_cmp_fm(Q):
            for s in range(2):
                col0 = 512 + 128 * s
                p = c.ps()
                for kt in range(8):
                    c.mm(p[:, 0:Q], win[:, kt, col0:col0 + 128], uT[:, kt, :Q], kt == 0, kt == 7, [win, uT], [p])
                for kt in range(8):
                    c.mm(p[0:64, Q:2 * Q], win[:, kt, col0 + 64:col0 + 128], uT[:, kt, :Q], kt == 0, kt == 7, [win, uT], [p])
                for kt in range(8):
                    c.mm(p[64:128, Q:2 * Q], win[:, kt, col0:col0 + 64], uT[:, kt, :Q], kt == 0, kt == 7, [win, uT], [p])
                c.op("act", lambda e, p=p, s=s: e.activation(X[s][0][0:64, 16:16 + Q], p[0:64, 0:Q], AF.Copy), [p], [X[s][0]])
                c.op("act", lambda e, p=p, s=s: e.activation(X[s][0][64:128, 15:15 + Q], p[64:128, Q:2 * Q], AF.Copy), [p], [X[s][0]])
                c.op("act", lambda e, p=p, s=s: e.activation(X[s][1][0:64, 16:16 + Q], p[0:64, Q:2 * Q], AF.Copy), [p], [X[s][1]])
                c.op("act", lambda e, p=p, s=s: e.activation(X[s][1][64:128, 15:15 + Q], p[64:128, 0:Q], AF.Copy), [p], [X[s][1]])

        load_mod(MODsh, l, 0, 0, 128); load_mod(MODsc, l, 1, 0, 128); load_mod(MODg, l, 2, 0, 128)
        for (row0, Q, sidx, i) in ptiles:
            c.dma("sp", xt[:Q, :], xsrc(D, l, row0, Q, sidx, SEQ), writes=[xt], owner=xt)
            outs = [(D["cmp_p"][l, row0:row0 + Q, :], 0), (D["slc_p"][l, row0:row0 + Q, :], 256)]
            WP = min(512, SEQ)
            if row0 >= SEQ - WP:
                outs.append((D["win_p"][l, row0 - (SEQ - WP):row0 - (SEQ - WP) + Q, :], 512))
            STG = int(os.environ.get('A2_STOP', '99'))
            if STG == 0:
                c.barrier(); return
            project(i, Q, xt, outs)
            if STG == 1:
                c.barrier(); return
            if i > 0:
                carry_X()
            project_cmp_fm(Q)
            if STG == 2:
                c.barrier(); return
            compress_tile(i)
            if STG == 3:
                c.barrier(); return
            attend(i, Q, 8 * i + 8, K["bc"], 2 * i, K["jt"], i >= 8)
            if STG == 4:
                c.barrier(); return
            finish_tile(row0, Q, xt)

        for (row0, Q, sidx, _) in ([] if os.environ.get('SKIP_SA2') else stiles):
            s = sidx - 1
            srow = row0 - SEQ
            load_mod(MODsh, l, 0, sidx, Q); load_mod(MODsc, l, 1, sidx, Q); load_mod(MODg, l, 2, sidx, Q)
            for t in [hidv, kcT]:
                c.op("pool", lambda e, t=t: e.memset(t[:], 0.0), [], [t])
            for g in range(2):
                c.op("pool", lambda e, g=g: e.memset(Pg[g][:], 0.0), [], [Pg[g]])
            c.dma("sp", idxi[:, :], D["ptab"][s:s + 1, :].partition_broadcast(128), writes=[idxi], owner=idxi)
            c.op("dve", lambda e: e.tensor_copy(idxf[:, :], idxi[:, :]), [idxi], [idxf])
            c.op("dve", lambda e: e.tensor_scalar(idxf[:, :], idxf[:, :], 128.0, pidx[:, 0:1], ALU.mult, ALU.add), [idxf, pidx], [idxf])
            c.op("dve", lambda e: e.tensor_scalar(idxf[:, :], idxf[:, :], float(l * NPHYS * 128), None, ALU.add), [idxf], [idxf])
            c.op("dve", lambda e: e.tensor_copy(idxi[:, :], idxf[:, :]), [idxf], [idxi])
            it = 0
            for j in range(16):
                gb = gbuf[it % 2]; it += 1
                c.dma("pool", None, None, reads=[idxi], writes=[gb], owner=gb,
                      fn=lambda e, gb=gb, j=j: e.indirect_dma_start(out=gb[:, :], out_offset=None, in_=D["pool_c"][:].rearrange("l r f -> (l r) f"),
                                                                     in_offset=bass.IndirectOffsetOnAxis(ap=idxi[:, j:j + 1], axis=0)))
                c.op("dve", lambda e, gb=gb: e.tensor_copy(gb16[:, 0, :], gb[:, :]), [gb], [gb16])
                for gg in range(2):
                    c.op("dve", lambda e, gb=gb, gg=gg: e.tensor_copy(gb16[:, 1, :].rearrange("p (s g d) -> p s g d", s=2, g=2)[:, :, gg, :],
                                                                    gb[:, :].rearrange("p (s g d) -> p s g d", s=2, g=2)[:, :, 1 - gg, :]), [gb], [gb16])
                if j > 0:
                    carry_X()
                p = c.ps()
                pb = p[:, :].bitcast(BF16)
                for w in range(2):
                    for s_ in range(2):
                        c.tr(pb[:, (w * 2 + s_) * 128:(w * 2 + s_ + 1) * 128], gb16[:, w, s_ * 128:(s_ + 1) * 128], K["identb"][:, :], [gb16, K["identb"]], [p])
                for s_ in range(2):
                    nat = pb[:, (0 * 2 + s_) * 128:(0 * 2 + s_ + 1) * 128]
                    sw = pb[:, (1 * 2 + s_) * 128:(1 * 2 + s_ + 1) * 128]
                    c.op("act", lambda e, nat=nat, s_=s_: e.activation(X[s_][0][0:64, 16:144], nat[0:64, :], AF.Copy), [p], [X[s_][0]])
                    c.op("act", lambda e, sw=sw, s_=s_: e.activation(X[s_][0][64:128, 15:143], sw[64:128, :], AF.Copy), [p], [X[s_][0]])
                    c.op("act", lambda e, sw=sw, s_=s_: e.activation(X[s_][1][0:64, 16:144], sw[0:64, :], AF.Copy), [p], [X[s_][1]])
                    c.op("act", lambda e, nat=nat, s_=s_: e.activation(X[s_][1][64:128, 15:143], nat[64:128, :], AF.Copy), [p], [X[s_][1]])
                compress_tile(j)
                gb = gbuf[it % 2]; it += 1
                c.dma("pool", None, None, reads=[idxi], writes=[gb], owner=gb,
                      fn=lambda e, gb=gb, j=j: e.indirect_dma_start(out=gb[:, :], out_offset=None, in_=D["pool_s"][:].rearrange("l r f -> (l r) f"),
                                                                     in_offset=bass.IndirectOffsetOnAxis(ap=idxi[:, j:j + 1], axis=0)))
                c.op("dve", lambda e, gb=gb: e.tensor_copy(gb16[:, 0, :], gb[:, :]), [gb], [gb16])
                p = c.ps()
                pb = p[:, :].bitcast(BF16)
                c.tr(pb[:, 0:128], gb16[:, 0, 0:128], K["identb"][:, :], [gb16, K["identb"]], [p])
                c.op("act", lambda e, pb=pb, j=j: e.activation(slcK[:, j * 128:(j + 1) * 128], pb[:, 0:128], AF.Copy), [p], [slcK])
                c.op("dve", lambda e, j=j: e.tensor_copy(slcV[:, j, :], gb16[:, 0, 128:256]), [gb16], [slcV])
            for j in range(4):
                kt = 12 + j
                gb = gbuf[it % 2]; it += 1
                c.dma("sp", gb[:, :], D["win_c"][l, s, j * 128:(j + 1) * 128, :], writes=[gb], owner=gb)
                c.op("dve", lambda e, gb=gb: e.tensor_copy(gb16[:, 0, :], gb[:, :]), [gb], [gb16])
                p = c.ps()
                pb = p[:, :].bitcast(BF16)
                c.tr(pb[:, 0:128], gb16[:, 0, 0:128], K["identb"][:, :], [gb16, K["identb"]], [p])
                c.op("act", lambda e, pb=pb, kt=kt: e.activation(winK[:, kt % 8, :], pb[:, 0:128], AF.Copy), [p], [winK])
                c.op("dve", lambda e, kt=kt: e.tensor_copy(winV[:, kt % 8, :], gb16[:, 0, 128:256]), [gb16], [winV])
            c.dma("pool", D["win_s"][l, s, 0:508, :], D["win_c"][l, s, 4:512, :], owner=gbuf[0])
            c.dma("sp", xt[:Q, :], xsrc(D, l, row0, Q, sidx, SEQ), writes=[xt], owner=xt)
            outs = [(D["cmp_s"][l, srow:srow + Q, :], 0), (D["slc_s"][l, srow:srow + Q, :], 256), (D["win_s"][l, s, 508:512, :], 512)]
            project(16, Q, xt, outs)
            attend(16, Q, 127, K["bcs"], 32, K["js"], True)
            finish_tile(row0, Q, xt)
        c.barrier()
```
